# Optimizing a Trainium2 kernel written in Bass

```python
import math
import jax, jax.numpy as jnp
from jax import lax
import numpy as np

D_MODEL = 1024
BATCH = 4
SEQ = 4096
DEPTH = 4
DEC_BATCH = 8
DEC_SEQ = 16
PAST_LEN = 2048

CHUNK = 64
Q_BLOCK = 128
EPS = 1e-6
NEG_INF = -1e30

MLA_HEADS = 8
MLA_Q_RANK = 256
MLA_KV_RANK = 128
MLA_NOPE = 64
MLA_ROPE = 32
MLA_V = 64
MLA_WIDTH = MLA_HEADS * MLA_V
ROPE_BASE = 10000.0

DIFF_HEADS = 8
DIFF_D = 32
DIFF_V = 2 * DIFF_D
DIFF_WIDTH = DIFF_HEADS * DIFF_V

LRU_WIDTH = 512
LRU_BLOCKS = 8
LRU_BW = LRU_WIDTH // LRU_BLOCKS
CONV_W = 4
LRU_C = 8.0

N_BRANCH = 3
SPLITS = (MLA_Q_RANK, MLA_KV_RANK, MLA_ROPE, MLA_WIDTH,
          DIFF_WIDTH, DIFF_WIDTH, DIFF_WIDTH, DIFF_WIDTH,
          LRU_WIDTH, LRU_WIDTH, N_BRANCH * D_MODEL)
N_IN = sum(SPLITS)

kernel_name = 'hybrid_mla_diff_rglru_stream_step'


def rms_norm(x, g):
    xf = x.astype(jnp.float32)
    y = xf * lax.rsqrt(jnp.mean(xf * xf, axis=-1, keepdims=True) + EPS)
    return (y * g.astype(jnp.float32)).astype(x.dtype)


def rope(x, pos):
    half = x.shape[-1] // 2
    inv = ROPE_BASE ** (-jnp.arange(half, dtype=jnp.float32) / half)
    ang = pos.astype(jnp.float32)[:, None] * inv[None, :]
    cos = jnp.cos(ang)[:, None, :]
    sin = jnp.sin(ang)[:, None, :]
    x1 = x[..., :half].astype(jnp.float32)
    x2 = x[..., half:].astype(jnp.float32)
    return jnp.concatenate([x1 * cos - x2 * sin, x2 * cos + x1 * sin], axis=-1).astype(x.dtype)


def alibi_slopes(n):
    return 2.0 ** (-8.0 * jnp.arange(1, n + 1, dtype=jnp.float32) / n)


def chunk_visible(q_pos, k_pos):
    return (k_pos[None, :] // CHUNK) <= (q_pos[:, None] // CHUNK)


def map_query_blocks(fn, q_arrays, q_pos):
    T = q_pos.shape[0]
    if T <= Q_BLOCK:
        return fn(q_arrays, q_pos)
    nb = T // Q_BLOCK
    blocks = tuple(jnp.moveaxis(a.reshape(a.shape[0], nb, Q_BLOCK, *a.shape[2:]), 1, 0) for a in q_arrays)
    out = lax.map(lambda args: fn(args[0], args[1]), (blocks, q_pos.reshape(nb, Q_BLOCK)))
    out = jnp.moveaxis(out, 0, 1)
    return out.reshape(out.shape[0], T, *out.shape[3:])


def rg_lru(xb, z, q_pos, conv0, h0, lp):
    B, T, W = xb.shape
    conv_in = jnp.concatenate([conv0, xb], axis=1)
    conv_new = conv_in[:, -(CONV_W - 1):]
    xc = lp['lru_conv_b'] + sum(conv_in[:, k:k + T] * lp['lru_conv_w'][k] for k in range(CONV_W))
    xcb = xc.reshape(B, T, LRU_BLOCKS, LRU_BW)
    r = jax.nn.sigmoid(jnp.einsum('btni,nio->btno', xcb, lp['lru_w_a']).reshape(B, T, W).astype(jnp.float32)
                       + lp['lru_b_a'].astype(jnp.float32))
    i = jax.nn.sigmoid(jnp.einsum('btni,nio->btno', xcb, lp['lru_w_x']).reshape(B, T, W).astype(jnp.float32)
                       + lp['lru_b_x'].astype(jnp.float32))
    log_a = -LRU_C * r * jax.nn.softplus(-lp['lru_lambda'].astype(jnp.float32))
    a = jnp.exp(log_a)
    mult = jnp.where((q_pos == 0)[None, :, None], 1.0, jnp.sqrt(-jnp.expm1(2.0 * log_a)))
    b = mult * i * xc.astype(jnp.float32)

    def combine(left, right):
        a1, b1 = left
        a2, b2 = right
        return a1 * a2, a2 * b1 + b2

    a_cum, b_cum = lax.associative_scan(combine, (a, b), axis=1)
    h = a_cum * h0.astype(jnp.float32)[:, None, :] + b_cum
    out = h.astype(xb.dtype) * jax.nn.silu(z)
    return out, h[:, -1].astype(xb.dtype), conv_new


def _layer(x, past, lp, layer_idx):
    ckv_past, krope_past, dk_past, dv_past, h0, conv0 = past
    B, T, _ = x.shape
    P = ckv_past.shape[1]
    q_pos = P + jnp.arange(T, dtype=jnp.int32)
    k_pos = jnp.arange(P + T, dtype=jnp.int32)

    xn = rms_norm(x, lp['norm'])
    proj = jnp.einsum('btd,dn->btn', xn, lp['w_in'])
    pieces = []
    off = 0
    for w in SPLITS:
        pieces.append(proj[..., off:off + w])
        off += w
    c_q, c_kv, k_r, z_mla, q_d, k_d, v_d, z_diff, x_lru, z_lru, gate_logits = pieces

    c_q = rms_norm(c_q, lp['mla_q_norm'])
    q = jnp.einsum('btr,rn->btn', c_q, lp['mla_w_uq']).reshape(B, T, MLA_HEADS, MLA_NOPE + MLA_ROPE)
    q_nope = q[..., :MLA_NOPE]
    q_rope = rope(q[..., MLA_NOPE:], q_pos)
    c_kv = rms_norm(c_kv, lp['mla_kv_norm'])
    k_r = rope(k_r[:, :, None, :], q_pos)[:, :, 0, :]
    ckv_all = jnp.concatenate([ckv_past, c_kv], axis=1)
    krope_all = jnp.concatenate([krope_past, k_r], axis=1)
    q_lat = jnp.einsum('bthd,chd->bthc', q_nope, lp['mla_w_uk'])
    mla_scale = (MLA_NOPE + MLA_ROPE) ** -0.5

    def mla_block(qs, qp):
        ql, qr = qs
        s = jnp.einsum('bthc,bsc->bhts', ql, ckv_all) + jnp.einsum('bthr,bsr->bhts', qr, krope_all)
        s = jnp.where(chunk_visible(qp, k_pos)[None, None], s.astype(jnp.float32) * mla_scale, NEG_INF)
        p = jax.nn.softmax(s, axis=-1).astype(ckv_all.dtype)
        return jnp.einsum('bhts,bsc->bthc', p, ckv_all)

    o_lat = map_query_blocks(mla_block, (q_lat, q_rope), q_pos)
    o_mla = jnp.einsum('bthc,chv->bthv', o_lat, lp['mla_w_uv']).reshape(B, T, MLA_WIDTH) * jax.nn.silu(z_mla)

    qd = q_d.reshape(B, T, DIFF_HEADS, 2, DIFF_D)
    kd = k_d.reshape(B, T, DIFF_HEADS, DIFF_V)
    vd = v_d.reshape(B, T, DIFF_HEADS, DIFF_V)
    kd_all = jnp.concatenate([dk_past, kd], axis=1).reshape(B, P + T, DIFF_HEADS, 2, DIFF_D)
    vd_all = jnp.concatenate([dv_past, vd], axis=1)
    lam_init = 0.8 - 0.6 * math.exp(-0.3 * layer_idx)
    lam = (jnp.exp(jnp.sum(lp['diff_lq1'].astype(jnp.float32) * lp['diff_lk1'].astype(jnp.float32)))
           - jnp.exp(jnp.sum(lp['diff_lq2'].astype(jnp.float32) * lp['diff_lk2'].astype(jnp.float32)))
           + lam_init)
    slopes = alibi_slopes(DIFF_HEADS)

    def diff_block(qs, qp):
        (qb,) = qs
        s = jnp.einsum('bthcd,bshcd->bchts', qb, kd_all).astype(jnp.float32) * DIFF_D ** -0.5
        dist = jnp.abs(qp[:, None] - k_pos[None, :]).astype(jnp.float32)
        s = s - slopes[:, None, None] * dist[None]
        s = jnp.where(chunk_visible(qp, k_pos)[None, None, None], s, NEG_INF)
        p = jax.nn.softmax(s, axis=-1)
        a = (p[:, 0] - lam * p[:, 1]).astype(vd_all.dtype)
        return jnp.einsum('bhts,bshv->bthv', a, vd_all)

    o_d = map_query_blocks(diff_block, (qd,), q_pos)
    o_d = rms_norm(o_d, lp['diff_subln']) * (1.0 - lam_init)
    o_diff = o_d.reshape(B, T, DIFF_WIDTH) * jax.nn.silu(z_diff)

    o_lru, h_last, conv_new = rg_lru(x_lru, z_lru, q_pos, conv0, h0, lp)

    g = jax.nn.sigmoid(gate_logits.astype(jnp.float32)).astype(x.dtype).reshape(B, T, N_BRANCH, D_MODEL)
    merged = (g[:, :, 0] * jnp.einsum('btw,wd->btd', o_mla, lp['w_o_mla'])
              + g[:, :, 1] * jnp.einsum('btw,wd->btd', o_diff, lp['w_o_diff'])
              + g[:, :, 2] * jnp.einsum('btw,wd->btd', o_lru, lp['w_o_lru']))
    y = jnp.einsum('btd,de->bte', merged, lp['w_out'])
    return x + y, (c_kv, k_r, kd, vd, h_last, conv_new)


def setup_inputs(seed: int = 0) -> dict:
    key = jax.random.key(seed)
    ks = jax.random.split(key, 32)

    def nrm(i, shape, scale):
        return jax.random.normal(ks[i], shape, jnp.float32) * scale

    def gain(i, shape):
        return 1.0 + nrm(i, shape, 0.02)

    a0 = jax.random.uniform(ks[30], (DEPTH, LRU_WIDTH), jnp.float32, 0.9, 0.999)
    return {
        'x_prompt': nrm(0, (BATCH, SEQ, D_MODEL), 1.0),
        'x_sample': nrm(1, (DEC_BATCH, DEC_SEQ, D_MODEL), 1.0),
        'cache_mla_ckv': nrm(2, (DEPTH, DEC_BATCH, PAST_LEN, MLA_KV_RANK), 1.0),
        'cache_mla_krope': nrm(3, (DEPTH, DEC_BATCH, PAST_LEN, MLA_ROPE), 1.0),
        'cache_diff_k': nrm(4, (DEPTH, DEC_BATCH, PAST_LEN, DIFF_HEADS, DIFF_V), 1.0),
        'cache_diff_v': nrm(5, (DEPTH, DEC_BATCH, PAST_LEN, DIFF_HEADS, DIFF_V), 1.0),
        'state_lru_h': nrm(6, (DEPTH, DEC_BATCH, LRU_WIDTH), 0.5),
        'state_lru_conv': nrm(7, (DEPTH, DEC_BATCH, CONV_W - 1, LRU_WIDTH), 1.0),
        'norm_g': gain(8, (DEPTH, D_MODEL)),
        'w_in': nrm(9, (DEPTH, D_MODEL, N_IN), D_MODEL ** -0.5),
        'mla_q_norm': gain(10, (DEPTH, MLA_Q_RANK)),
        'mla_kv_norm': gain(11, (DEPTH, MLA_KV_RANK)),
        'mla_w_uq': nrm(12, (DEPTH, MLA_Q_RANK, MLA_HEADS * (MLA_NOPE + MLA_ROPE)), MLA_Q_RANK ** -0.5),
        'mla_w_uk': nrm(13, (DEPTH, MLA_KV_RANK, MLA_HEADS, MLA_NOPE), MLA_KV_RANK ** -0.5),
        'mla_w_uv': nrm(14, (DEPTH, MLA_KV_RANK, MLA_HEADS, MLA_V), MLA_KV_RANK ** -0.5),
        'diff_lq1': nrm(15, (DEPTH, DIFF_D), 0.1),
        'diff_lk1': nrm(16, (DEPTH, DIFF_D), 0.1),
        'diff_lq2': nrm(17, (DEPTH, DIFF_D), 0.1),
        'diff_lk2': nrm(18, (DEPTH, DIFF_D), 0.1),
        'diff_subln': gain(19, (DEPTH, DIFF_V)),
        'lru_conv_w': nrm(20, (DEPTH, CONV_W, LRU_WIDTH), CONV_W ** -0.5),
        'lru_conv_b': nrm(21, (DEPTH, LRU_WIDTH), 0.01),
        'lru_w_a': nrm(22, (DEPTH, LRU_BLOCKS, LRU_BW, LRU_BW), LRU_BW ** -0.5),
        'lru_b_a': nrm(23, (DEPTH, LRU_WIDTH), 0.01),
        'lru_w_x': nrm(24, (DEPTH, LRU_BLOCKS, LRU_BW, LRU_BW), LRU_BW ** -0.5),
        'lru_b_x': nrm(25, (DEPTH, LRU_WIDTH), 0.01),
        'lru_lambda': jnp.log(a0) - jnp.log1p(-a0),
        'w_o_mla': nrm(26, (DEPTH, MLA_WIDTH, D_MODEL), MLA_WIDTH ** -0.5),
        'w_o_diff': nrm(27, (DEPTH, DIFF_WIDTH, D_MODEL), DIFF_WIDTH ** -0.5),
        'w_o_lru': nrm(28, (DEPTH, LRU_WIDTH, D_MODEL), LRU_WIDTH ** -0.5),
        'w_out': nrm(29, (DEPTH, D_MODEL, D_MODEL), D_MODEL ** -0.5),
        'final_norm': gain(31, (D_MODEL,)),
    }


def _stacked(states, i):
    return jnp.stack([s[i] for s in states], axis=0)


def reference(x_prompt, x_sample, cache_mla_ckv, cache_mla_krope, cache_diff_k, cache_diff_v,
              state_lru_h, state_lru_conv, norm_g, w_in, mla_q_norm, mla_kv_norm, mla_w_uq, mla_w_uk,
              mla_w_uv, diff_lq1, diff_lk1, diff_lq2, diff_lk2, diff_subln, lru_conv_w, lru_conv_b,
              lru_w_a, lru_b_a, lru_w_x, lru_b_x, lru_lambda, w_o_mla, w_o_diff, w_o_lru, w_out,
              final_norm):
    bp = x_prompt.shape[0]
    dt = x_prompt.dtype
    empty_past = (jnp.zeros((bp, 0, MLA_KV_RANK), dt), jnp.zeros((bp, 0, MLA_ROPE), dt),
                  jnp.zeros((bp, 0, DIFF_HEADS, DIFF_V), dt), jnp.zeros((bp, 0, DIFF_HEADS, DIFF_V), dt),
                  jnp.zeros((bp, LRU_WIDTH), dt), jnp.zeros((bp, CONV_W - 1, LRU_WIDTH), dt))
    prompt_states = []
    sample_states = []
    xp = x_prompt
    xs = x_sample
    for l in range(DEPTH):
        lp = {
            'norm': norm_g[l], 'w_in': w_in[l],
            'mla_q_norm': mla_q_norm[l], 'mla_kv_norm': mla_kv_norm[l],
            'mla_w_uq': mla_w_uq[l], 'mla_w_uk': mla_w_uk[l], 'mla_w_uv': mla_w_uv[l],
            'diff_lq1': diff_lq1[l], 'diff_lk1': diff_lk1[l], 'diff_lq2': diff_lq2[l], 'diff_lk2': diff_lk2[l],
            'diff_subln': diff_subln[l],
            'lru_conv_w': lru_conv_w[l], 'lru_conv_b': lru_conv_b[l],
            'lru_w_a': lru_w_a[l], 'lru_b_a': lru_b_a[l], 'lru_w_x': lru_w_x[l], 'lru_b_x': lru_b_x[l],
            'lru_lambda': lru_lambda[l],
            'w_o_mla': w_o_mla[l], 'w_o_diff': w_o_diff[l], 'w_o_lru': w_o_lru[l], 'w_out': w_out[l],
        }
        xp, st_p = _layer(xp, empty_past, lp, l)
        past = (cache_mla_ckv[l], cache_mla_krope[l], cache_diff_k[l], cache_diff_v[l],
                state_lru_h[l], state_lru_conv[l])
        xs, st_s = _layer(xs, past, lp, l)
        prompt_states.append(st_p)
        sample_states.append(st_s)
    y_prompt = rms_norm(xp, final_norm)
    y_sample = rms_norm(xs, final_norm)
    return (y_prompt, y_sample,
            _stacked(prompt_states, 0), _stacked(prompt_states, 1), _stacked(prompt_states, 2),
            _stacked(prompt_states, 3), _stacked(prompt_states, 4), _stacked(prompt_states, 5),
            _stacked(sample_states, 0), _stacked(sample_states, 1), _stacked(sample_states, 2),
            _stacked(sample_states, 3), _stacked(sample_states, 4), _stacked(sample_states, 5))
```

```python
import contextlib
import math
import numpy as np
import concourse.bass as bass
import concourse.mybir as mybir
from concourse.bass_utils import run_bass_kernel_spmd

F32 = mybir.dt.float32
BF16 = mybir.dt.bfloat16
AF = mybir.ActivationFunctionType
ALU = mybir.AluOpType
AX = mybir.AxisListType

D = 1024
SEQ = 4096
DEPTH = 4
DEC_SEQ = 16
PAST = 2048
NIN = 7072
EPS = 1e-6
TT = 256
QSCALE = 96 ** -0.5
DSCALE = 32 ** -0.5
C_CQ, C_CKV, C_KR, C_ZM, C_QD, C_KD, C_VD, C_ZD, C_XL, C_ZL, C_G = (
    0, 256, 384, 416, 928, 1440, 1952, 2464, 2976, 3488, 4000)
NV = 8 + 2 + 128 + 512 + 128 + 16 + 16
V_G, V_QG, V_KVG, V_SUB, V_LAM, V_CW, V_MISC = 0, 8, 10, 138, 650, 778, 794

ENGS = ("pe", "act", "dve", "pool", "sp")
NDMA_SEMS = 24


class Res:
    __slots__ = ("name", "last_w", "readers", "dma_readers", "excl")

    def __init__(self, name, excl=False):
        self.name = name
        self.last_w = None
        self.readers = {}
        self.dma_readers = []
        self.excl = excl


class Op:
    __slots__ = ("eng", "fn", "deps", "idx", "is_dma", "dma_n", "signal", "sigcount")

    def __init__(self, eng, fn, is_dma):
        self.eng = eng
        self.fn = fn
        self.deps = set()
        self.is_dma = is_dma
        self.signal = False
        self.sigcount = 0
        self.dma_n = -1


class Prog:
    def __init__(self, nc):
        self.nc = nc
        self.ops = {e: [] for e in ENGS}

    def res(self, name="", excl=False):
        return Res(name, excl)

    def op(self, eng, fn, reads=(), writes=(), dma=False):
        o = Op(eng, fn, dma)
        o.idx = len(self.ops[eng])
        reads = list(reads)
        writes = list(writes)
        for r in list(reads):
            if r.excl and r not in writes:
                writes.append(r)
        for r in reads:
            if r.last_w is not None:
                o.deps.add(r.last_w)
        for w in writes:
            if w.last_w is not None:
                o.deps.add(w.last_w)
            for rd in w.readers.values():
                o.deps.add(rd)
            for rd in w.dma_readers:
                o.deps.add(rd)
        for r in reads:
            if r in writes:
                continue
            if dma:
                r.dma_readers.append(o)
                if len(r.dma_readers) > 3 * NDMA_SEMS:
                    r.dma_readers = r.dma_readers[-2 * NDMA_SEMS:]
            else:
                r.readers[eng] = o
        for w in writes:
            w.last_w = o
            w.readers = {}
            w.dma_readers = []
        o.deps.discard(o)
        self.ops[eng].append(o)
        return o

    def emit(self):
        nc = self.nc
        dma_count = {e: 0 for e in ENGS}
        for e in ENGS:
            for o in self.ops[e]:
                if o.is_dma:
                    o.dma_n = dma_count[e]
                    dma_count[e] += 1
        for e in ENGS:
            for o in self.ops[e]:
                for d in o.deps:
                    if d.is_dma:
                        continue
                    if d.eng == o.eng and d.eng == "pe" and not o.is_dma:
                        continue
                    d.signal = True
        for e in ENGS:
            c = 0
            for o in self.ops[e]:
                if o.signal and not o.is_dma:
                    c += 1
                o.sigcount = c
        stack = contextlib.ExitStack()
        sems = {e: stack.enter_context(nc.semaphore(f"s_{e}")) for e in ENGS}
        dsems = {e: [stack.enter_context(nc.semaphore(f"d_{e}{i}")) for i in range(NDMA_SEMS)]
                 for e in ENGS if dma_count[e] > 0}
        block = stack.enter_context(nc.Block())
        engobj = {"pe": "tensor", "act": "scalar", "dve": "vector", "pool": "gpsimd", "sp": "sync"}
        stats = {}

        def make_body(e):
            def body(eng):
                seen = {x: 0 for x in ENGS}
                seen_dma = {}
                nwait = 0
                ops = self.ops[e]
                ndma = dma_count[e]
                for o in ops:
                    need = {}
                    need_dma = {}
                    for d in o.deps:
                        if d.is_dma:
                            key = (d.eng, d.dma_n % NDMA_SEMS)
                            val = 16 * (d.dma_n // NDMA_SEMS + 1)
                            if seen_dma.get(key, 0) < val:
                                need_dma[key] = max(need_dma.get(key, 0), val)
                        else:
                            if d.eng == e and e == "pe" and not o.is_dma:
                                continue
                            if seen[d.eng] < d.sigcount:
                                need[d.eng] = max(need.get(d.eng, 0), d.sigcount)
                    if o.is_dma and o.dma_n >= NDMA_SEMS:
                        key = (e, o.dma_n % NDMA_SEMS)
                        val = 16 * (o.dma_n // NDMA_SEMS)
                        if seen_dma.get(key, 0) < val:
                            need_dma[key] = max(need_dma.get(key, 0), val)
                    for x, v in need.items():
                        eng.wait_ge(sems[x], v)
                        seen[x] = v
                        nwait += 1
                    for key, v in need_dma.items():
                        eng.wait_ge(dsems[key[0]][key[1]], v)
                        seen_dma[key] = v
                        nwait += 1
                    ins = o.fn(eng)
                    if o.is_dma:
                        ins.then_inc(dsems[e][o.dma_n % NDMA_SEMS], 16)
                    elif o.signal:
                        ins.then_inc(sems[e], 1)
                for slot in range(min(ndma, NDMA_SEMS)):
                    last = ((ndma - 1 - slot) // NDMA_SEMS) * NDMA_SEMS + slot
                    val = 16 * (last // NDMA_SEMS + 1)
                    if seen_dma.get((e, slot), 0) < val:
                        eng.wait_ge(dsems[e][slot], val)
                stats[e] = (len(ops), nwait)
            return body

        for e in ENGS:
            if self.ops[e]:
                getattr(block, engobj[e])(make_body(e))
        stack.close()
        return stats


def build_program(depth=DEPTH, seq=SEQ, do_sample=True):
    nc = bass.Bass("TRN2", target_bir_lowering=False, dynamic_dma_scratch_size=4096)
    P = Prog(nc)
    NBLK = max(seq, PAST + 128 if do_sample else 0) // 128
    SCAP = NBLK * 128

    def din(name, shape):
        return nc.dram_tensor(name, list(shape), F32, kind="ExternalInput").ap()

    def dout(name, shape):
        return nc.dram_tensor(name, list(shape), F32, kind="ExternalOutput").ap()

    x_p = din("x_p", [seq, D])
    x_s = din("x_s", [DEC_SEQ, D])
    c_ckv = din("c_ckv", [DEPTH, PAST, 128])
    c_kr = din("c_kr", [DEPTH, PAST, 32])
    c_dk = din("c_dk", [DEPTH, PAST, 512])
    c_dv = din("c_dv", [DEPTH, PAST, 512])
    svec = din("svec", [DEPTH, 128, 16])
    vecs = din("vecs", [DEPTH, 128, NV])
    fnorm = din("fnorm", [128, D])
    w_in = din("w_in", [DEPTH, 14, 128, 4096])
    w_o = din("w_o", [DEPTH, 3, 128, 4096])
    w_out = din("w_out", [DEPTH, 2, 128, 4096])
    w_uq = din("w_uq", [DEPTH, 128, 2, 768])
    w_ukT = din("w_ukT", [DEPTH, 64, 8, 128])
    w_uv = din("w_uv", [DEPTH, 128, 8, 128])
    bd_a = din("bd_a", [DEPTH, 128, 4, 128])
    bd_x = din("bd_x", [DEPTH, 128, 4, 128])
    ident_d = din("ident", [128, 128])
    rope_d = din("rope", [128, 2, 32, 16])
    btab_d = din("btab", [128, 8, 33])
    mask_d = din("masks", [128, 9, 128])

    y_p = dout("y_p", [seq, D])
    y_s = dout("y_s", [DEC_SEQ, D])
    o_ckv = {0: dout("p_ckv", [DEPTH, seq, 128]), 1: dout("s_ckv", [DEPTH, DEC_SEQ, 128])}
    o_kr = {0: dout("p_kr", [DEPTH, seq, 32]), 1: dout("s_kr", [DEPTH, DEC_SEQ, 32])}
    o_dk = {0: dout("p_dk", [DEPTH, seq, 512]), 1: dout("s_dk", [DEPTH, DEC_SEQ, 512])}
    o_dv = {0: dout("p_dv", [DEPTH, seq, 512]), 1: dout("s_dv", [DEPTH, DEC_SEQ, 512])}
    o_h = {0: dout("p_h", [DEPTH, 512]), 1: dout("s_h", [DEPTH, 512])}
    o_conv = {0: dout("p_conv", [DEPTH, 3, 512]), 1: dout("s_conv", [DEPTH, 3, 512])}

    xres = {0: nc.dram_tensor("xres_p", [seq, D], F32, kind="Internal").ap(),
            1: nc.dram_tensor("xres_s", [DEC_SEQ, D], F32, kind="Internal").ap()}
    wb_in = nc.dram_tensor("wb_in", [DEPTH, 14, 128, 4096], BF16, kind="Internal").ap()
    wb_o = nc.dram_tensor("wb_o", [DEPTH, 3, 128, 4096], BF16, kind="Internal").ap()
    wb_out = nc.dram_tensor("wb_out", [DEPTH, 2, 128, 4096], BF16, kind="Internal").ap()

    def sb(name, shape, dt=F32):
        return nc.alloc_sbuf_tensor("sb_" + name, list(shape), dt)

    ckvT = sb("ckvT", [128, SCAP], BF16)
    kropeT = sb("kropeT", [128, SCAP], BF16)
    ckv_aug = sb("ckv_aug", [128, NBLK, 130], BF16)
    KdT = sb("KdT", [128, 4, SCAP], BF16)
    Vd = sb("Vd", [128, NBLK, 8, 66], BF16)
    Rkv = [P.res(f"kv{b}") for b in range(NBLK)]
    ident = sb("ident16", [128, 128], BF16)
    ropeb = [sb(f"rope{i}", [128, 2, 2, 16]) for i in range(2)]
    Rropeb = [P.res(f"rope{i}") for i in range(2)]
    rope_n = [0]
    btab = sb("btab", [128, 8, 33])
    masks = sb("masks16", [128, 9, 128], BF16)
    fnorm_t = sb("fnorm", [128, D])
    Rconst = P.res("const")
    vec = sb("vecs", [128, NV])
    sv = sb("svec", [128, 16])
    lrv = sb("lrv", [128, 16])
    lamv = sb("lamv", [128, 4])
    Rvec = P.res("vec")
    wuq = sb("wuq", [128, 2, 768], BF16)
    wukT = sb("wukT", [128, 8, 128], BF16)
    wuv = sb("wuv", [128, 8, 128], BF16)
    bda = sb("bda", [128, 4, 128], BF16)
    bdx = sb("bdx", [128, 4, 128], BF16)
    Rlw = P.res("layerw")

    RING = 4
    ring = [sb(f"ring{i}", [128, 4096], BF16) for i in range(RING)]
    Rring = [P.res(f"ring{i}") for i in range(RING)]
    ring_n = [0]
    xnT = sb("xnT", [128, 8, TT], BF16)
    RxnT = P.res("xnT")
    qT = sb("qT", [128, 8, TT], BF16)
    RqT = P.res("qT")
    cqnT = sb("cqnT", [128, 2, TT], BF16)
    RcqnT = P.res("cqnT")
    qlatT = [sb(f"qlatT{i}", [128, TT], BF16) for i in range(2)]
    RqlatT = [P.res(f"qlatT{i}") for i in range(2)]
    qdT = sb("qdT", [128, 4, 4 * TT], BF16)
    RqdT = P.res("qdT")
    olruT = sb("olruT", [128, 4, TT], BF16)
    RolruT = P.res("olruT")
    omlaT = sb("omlaT", [128, 4, TT], BF16)
    RomlaT = P.res("omlaT")
    odiffT = sb("odiffT", [128, 4, TT], BF16)
    RodiffT = P.res("odiffT")
    zmlaT = sb("zmlaT", [128, 4, TT], BF16)
    RzmlaT = P.res("zmlaT")
    zlruT = sb("zlruT", [128, 4, TT], BF16)
    RzlruT = P.res("zlruT")
    zdiff = sb("zdiff", [128, TT // 128, 512], BF16)
    Rzdiff = P.res("zdiff")
    NPT = 4
    PT = [sb(f"PT{i}", [128, 2 * TT], BF16) for i in range(NPT)]
    RPT = [P.res(f"PT{i}") for i in range(NPT)]
    pt_n = [0]
    xl = sb("xl", [128, 4, TT + 3])
    Rxl = P.res("xl")
    hcarry = sb("hcarry", [128, 4])
    Rhc = P.res("hcarry")
    od_raw = sb("od_raw", [128, TT // 128, 512])
    Rodraw = P.res("odraw")
    olatT = [sb(f"olatT{i}", [128, TT], BF16) for i in range(2)]
    RolatT = [P.res(f"olatT{i}") for i in range(2)]
    NSCR = 6
    scr = [sb(f"scr{i}", [128, 512]) for i in range(NSCR)]
    Rscr = [P.res(f"scr{i}") for i in range(NSCR)]
    scr_n = [0]
    g_n = [0]
    small = [sb(f"small{i}", [128, 16]) for i in range(8)]
    Rsmall = [P.res(f"small{i}") for i in range(8)]
    small_n = [0]
    xblk = [sb(f"xblk{i}", [128, D]) for i in range(2)]
    Rxblk = [P.res(f"xblk{i}") for i in range(2)]
    xblk_n = [0]
    xs16 = [sb(f"xs16_{i}", [128, D], BF16) for i in range(1)] * 2
    Rxs16 = [P.res(f"xs16_{i}") for i in range(1)] * 2
    stg = [sb(f"stg{i}", [128, 1184]) for i in range(2)]
    Rstg = [P.res(f"stg{i}") for i in range(2)]
    stg_n = [0]
    tm16 = [sb(f"tm16_{i}", [128, 768], BF16) for i in range(2)]
    Rtm16 = [P.res(f"tm16_{i}") for i in range(2)]
    tm_n = [0]

    banks = [nc.alloc_psum_tensor(f"bank{i}", [128, 512], F32) for i in range(8)]
    Rbank = [P.res(f"bank{i}", excl=True) for i in range(8)]
    bank_n = [0]

    def next_bank(lo=0, hi=8):
        i = lo + bank_n[0] % (hi - lo)
        bank_n[0] += 1
        return i

    def nxt(counter, n):
        i = counter[0] % n
        counter[0] += 1
        return i

    def T_(fn, reads, writes):
        return P.op("pe", fn, reads, writes)

    def A_(fn, reads, writes):
        return P.op("act", fn, reads, writes)

    def V_(fn, reads, writes):
        return P.op("dve", fn, reads, writes)

    def G_(fn, reads, writes):
        return P.op("pool", fn, reads, writes)

    def LD(out, in_, reads, writes):
        return P.op("sp", lambda e, o=out, i=in_: e.dma_start(out=o, in_=i), reads, writes, dma=True)

    def ST(out, in_, reads, writes, slow=False):
        return P.op("pool", lambda e, o=out, i=in_: e.dma_start(out=o, in_=i, allow_slow_non_contiguous=slow),
                    reads, writes, dma=True)

    def CAST_DMA(out, in_, reads, writes):
        return P.op("pool", lambda e, o=out, i=in_: e.dma_start(out=o, in_=i, max_dma_last_dim=4096),
                    reads, writes, dma=True)

    evac_n = [0]

    def evac(out, in_, reads, writes, scale=None, eng=None):
        if eng is None:
            eng = "act" if evac_n[0] % 2 == 0 else "dve"
            evac_n[0] += 1
        if eng == "act":
            if scale is None:
                A_(lambda e, o=out, i=in_: e.activation(out=o, in_=i, func=AF.Copy), reads, writes)
            else:
                A_(lambda e, o=out, i=in_, s=scale: e.activation(out=o, in_=i, func=AF.Copy, scale=s), reads, writes)
        else:
            if scale is None:
                V_(lambda e, o=out, i=in_: e.tensor_copy(out=o, in_=i), reads, writes)
            else:
                V_(lambda e, o=out, i=in_, s=scale: e.tensor_scalar(out=o, in0=i, scalar1=s, scalar2=None, op0=ALU.mult),
                   reads, writes)

    def mm_chain(out, pairs, reads, writes):
        n = len(pairs)
        for i, (l, r) in enumerate(pairs):
            T_(lambda e, o=out, l=l, r=r, st=(i == 0), sp=(i == n - 1): e.matmul(o, lhsT=l, rhs=r, start=st, stop=sp),
               reads, writes)

    def transpose(out, in_, rows, reads, writes):
        T_(lambda e, o=out, i=in_, r=rows: e.transpose(out=o, in_=i, identity=ident[:r, :r]), list(reads) + [Rconst], writes)

    def bf(bank_ap_tensor):
        return bank_ap_tensor[:, :].bitcast(BF16)

    def rstd_from_ssq(ssq_ap, n, rs_res):
        A_(lambda e, a=ssq_ap: e.activation(out=a, in_=a, func=AF.Ln, scale=1.0 / n, bias=EPS), [rs_res], [rs_res])
        A_(lambda e, a=ssq_ap: e.activation(out=a, in_=a, func=AF.Exp, scale=-0.5), [rs_res], [rs_res])

    CAST_DMA(ident[:, :], ident_d, [], [Rconst])
    CAST_DMA(masks[:, :, :], mask_d, [], [Rconst])
    LD(btab[:, :, :], btab_d, [], [Rconst])
    LD(fnorm_t[:, :], fnorm, [], [Rconst])
    Rwb = {}
    for l in range(depth):
        Rwb[("in", l)] = P.res(f"wbin{l}")
        Rwb[("o", l)] = P.res(f"wbo{l}")
        Rwb[("out", l)] = P.res(f"wbout{l}")
    G_(lambda e: e.memset(ckv_aug[:, :, 128:130], 1.0), [], Rkv)
    G_(lambda e: e.memset(Vd[:, :, :, 64:66], 1.0), [], Rkv)
    G_(lambda e: e.memset(qdT[:, :, :], 0.0), [], [RqdT])
    G_(lambda e: e.memset(kropeT[:, :], 0.0), [], Rkv)
    G_(lambda e: e.memset(wukT[:, :, :], 0.0), [], [Rlw])
    G_(lambda e: e.memset(qT[:, :, :], 0.0), [], [RqT])

    def cast_layer_weights(l):
        for p in range(14):
            CAST_DMA(wb_in[l, p], w_in[l, p], [], [Rwb[("in", l)]])
        for b in range(3):
            CAST_DMA(wb_o[l, b], w_o[l, b], [], [Rwb[("o", l)]])
        for eh in range(2):
            CAST_DMA(wb_out[l, eh], w_out[l, eh], [], [Rwb[("out", l)]])

    cast_layer_weights(0)

    def load_panel(kind, l, sel, slot):
        i = slot
        t = ring[i]
        if kind == "in":
            c0, nco = sel
            pidx = 0 if c0 == 0 else 1 + (c0 - 416) // 512
            LD(t[:, 0:8 * nco], wb_in[l, pidx, :, 0:8 * nco], [Rwb[("in", l)]], [Rring[i]])
            view = t[:, 0:8 * nco].rearrange("p (a n) -> p a n", a=8)
        elif kind == "o":
            LD(t[:, :], wb_o[l, sel], [Rwb[("o", l)]], [Rring[i]])
            view = t[:, :].rearrange("p (a n) -> p a n", a=4)
        else:
            LD(t[:, :], wb_out[l, sel], [Rwb[("out", l)]], [Rring[i]])
            view = t[:, :].rearrange("p (a n) -> p a n", a=8)
        return view, Rring[i]

    def layer_setup(l):
        LD(vec[:, :], vecs[l], [], [Rvec])
        LD(sv[:, :], svec[l], [], [Rvec])
        CAST_DMA(wuq[:, :, :], w_uq[l], [], [Rlw])
        CAST_DMA(wukT[0:64, :, :], w_ukT[l], [], [Rlw])
        CAST_DMA(wuv[:, :, :], w_uv[l], [], [Rlw])
        CAST_DMA(bda[:, :, :], bd_a[l], [], [Rlw])
        CAST_DMA(bdx[:, :, :], bd_x[l], [], [Rlw])
        lam_init = 0.8 - 0.6 * math.exp(-0.3 * l)
        V_(lambda e: e.tensor_tensor(out=scr[0][:, 0:32], in0=vec[:, V_LAM:V_LAM + 32],
                                     in1=vec[:, V_LAM + 32:V_LAM + 64], op=ALU.mult), [Rvec], [Rscr[0]])
        V_(lambda e: e.tensor_reduce(out=lamv[:, 1:2], in_=scr[0][:, 0:32], axis=AX.X, op=ALU.add), [Rscr[0]], [Rvec])
        V_(lambda e: e.tensor_tensor(out=scr[0][:, 32:64], in0=vec[:, V_LAM + 64:V_LAM + 96],
                                     in1=vec[:, V_LAM + 96:V_LAM + 128], op=ALU.mult), [Rvec], [Rscr[0]])
        V_(lambda e: e.tensor_reduce(out=lamv[:, 2:3], in_=scr[0][:, 32:64], axis=AX.X, op=ALU.add), [Rscr[0]], [Rvec])
        A_(lambda e: e.activation(out=lamv[:, 1:3], in_=lamv[:, 1:3], func=AF.Exp), [Rvec], [Rvec])
        V_(lambda e, li=lam_init: e.scalar_tensor_tensor(out=lamv[:, 0:1], in0=lamv[:, 2:3], scalar=-li, in1=lamv[:, 1:2],
                                                         op0=ALU.add, op1=ALU.subtract), [Rvec], [Rvec])
        mo = V_MISC
        A_(lambda e: e.activation(out=lrv[:, 8:12], in_=vec[:, mo + 12:mo + 16], func=AF.Exp, scale=-1.0), [Rvec], [Rvec])
        A_(lambda e: e.activation(out=lrv[:, 8:12], in_=lrv[:, 8:12], func=AF.Ln, bias=1.0), [Rvec], [Rvec])
        V_(lambda e: e.tensor_scalar(out=lrv[:, 0:4], in0=lrv[:, 8:12], scalar1=-8.0, scalar2=None, op0=ALU.mult), [Rvec], [Rvec])
        V_(lambda e: e.tensor_scalar(out=lrv[:, 4:8], in0=lrv[:, 8:12], scalar1=-16.0, scalar2=None, op0=ALU.mult), [Rvec], [Rvec])
        return lam_init

    def kv_build_block(kb, r, st, Rst):
        i = nxt(tm_n, 2)
        t16, Rt = tm16[i], Rtm16[i]
        evac(t16[:r, 0:672], st[:r, 0:672], [Rst], [Rt])
        G_(lambda e: e.tensor_copy(out=ckv_aug[:r, kb, 0:128], in_=st[:r, 0:128]), [Rst], [Rkv[kb]])
        G_(lambda e: e.tensor_copy(out=Vd[:r, kb, :, 0:64], in_=st[:r, 672:1184].rearrange("p (h v) -> p h v", h=8)),
           [Rst], [Rkv[kb]])
        b = next_bank()
        pb = bf(banks[b])
        transpose(pb[:, 0:r], t16[:r, 0:128], r, [Rt], [Rbank[b]])
        transpose(pb[0:32, 128:128 + r], t16[:r, 128:160], r, [Rt], [Rbank[b]])
        for g in range(4):
            transpose(pb[:, 256 + 128 * g:256 + 128 * g + r], t16[:r, 160 + 128 * g:288 + 128 * g], r, [Rt], [Rbank[b]])
        c0 = kb * 128
        evac(ckvT[:, c0:c0 + r], pb[:, 0:r], [Rbank[b]], [Rkv[kb]])
        evac(kropeT[64:96, c0:c0 + r], pb[0:32, 128:128 + r], [Rbank[b]], [Rkv[kb]])
        evac(KdT[:, :, c0:c0 + r], pb[:, 256:768].rearrange("p (g s) -> p g s", g=4)[:, :, 0:r], [Rbank[b]], [Rkv[kb]])

    def rope_tm(dst16, src, r, blk, nh, w, tq, Rsrc, Rdst):
        si = nxt(scr_n, NSCR)
        s, Rs = scr[si], Rscr[si]
        rp, Rrp = blk
        cosb = rp[:r, 0, tq, :].unsqueeze(1).to_broadcast([r, nh, 16])
        sinb = rp[:r, 1, tq, :].unsqueeze(1).to_broadcast([r, nh, 16])
        x1 = src[:, :, w - 32:w - 16]
        x2 = src[:, :, w - 16:w]
        t = [s[:r, 128 * j:128 * j + nh * 16].rearrange("p (h d) -> p h d", h=nh) for j in range(4)]
        V_(lambda e: e.tensor_tensor(out=t[0], in0=x1, in1=cosb, op=ALU.mult), [Rsrc, Rrp], [Rs])
        V_(lambda e: e.tensor_tensor(out=t[1], in0=x2, in1=sinb, op=ALU.mult), [Rsrc, Rrp], [Rs])
        V_(lambda e: e.tensor_tensor(out=t[2], in0=x2, in1=cosb, op=ALU.mult), [Rsrc, Rrp], [Rs])
        V_(lambda e: e.tensor_tensor(out=t[3], in0=x1, in1=sinb, op=ALU.mult), [Rsrc, Rrp], [Rs])
        V_(lambda e: e.tensor_tensor(out=dst16[:, :, w - 32:w - 16], in0=t[0], in1=t[1], op=ALU.subtract), [Rs], [Rdst])
        V_(lambda e: e.tensor_tensor(out=dst16[:, :, w - 16:w], in0=t[2], in1=t[3], op=ALU.add), [Rs], [Rdst])

    def attention_tile(l, b0, nt, lam_init, bgen):
        nqs = (nt + 127) // 128
        rq = [min(128, nt - 128 * j) for j in range(nqs)]
        nkb = b0 + nqs
        tot_keys = b0 * 128 + nt

        def ks_of(kb):
            return min(128, tot_keys - kb * 128)

        def col_range(kb):
            j = kb - b0
            return (128 * j if j > 0 else 0), nt

        ACC = (0, 1)

        def mk_qlat(h):
            b = 6
            mm_chain(banks[b][:, 0:nt], [(wukT[:, h, :], qT[:, h, 0:nt])], [Rlw, RqT], [Rbank[b]])
            evac(qlatT[h % 2][:, 0:nt], banks[b][:, 0:nt], [Rbank[b]], [RqlatT[h % 2]])

        mk_qlat(0)
        pend = [None]
        for h in range(8):
            qi = h % 2
            if h + 1 < 8:
                mk_qlat(h + 1)
            started = [False]

            def mla_score(kb):
                sbk = next_bank(2, 6)
                ks = ks_of(kb)
                c0, c1 = col_range(kb)
                mm_chain(banks[sbk][:ks, 0:c1 - c0],
                         [(ckvT[:, kb * 128:kb * 128 + ks], qlatT[qi][:, c0:c1]),
                          (kropeT[:, kb * 128:kb * 128 + ks], qT[:, h, c0:c1])],
                         [Rkv[kb], RqlatT[qi], RqT], [Rbank[sbk]])
                return sbk

            squeue = [mla_score(k) for k in range(min(2, nkb))]
            if pend[0] is not None:
                pend[0]()
                pend[0] = None
            for kb in range(nkb):
                sbk = squeue.pop(0)
                if kb + 2 < nkb:
                    squeue.append(mla_score(kb + 2))
                ks = ks_of(kb)
                c0, c1 = col_range(kb)
                n = c1 - c0
                pi = nxt(pt_n, NPT)
                A_(lambda e, o=PT[pi][:ks, 0:n], i=banks[sbk][:ks, 0:n]: e.activation(out=o, in_=i, func=AF.Exp, scale=QSCALE),
                   [Rbank[sbk]], [RPT[pi]])
                if kb >= b0:
                    w = min(128, n)
                    V_(lambda e, o=PT[pi][:ks, 0:w], m=masks[:ks, 8, 0:w]: e.tensor_tensor(out=o, in0=o, in1=m, op=ALU.mult),
                       [RPT[pi], Rconst], [RPT[pi]])
                for qs in range(max(0, kb - b0), nqs):
                    ab = h % 2
                    col = qs * 129
                    st_ = not started[0]
                    started[0] = True
                    T_(lambda e, o=banks[ab][:rq[qs], col:col + 129], lt=PT[pi][:ks, qs * 128 - c0:qs * 128 - c0 + rq[qs]],
                       r_=ckv_aug[:ks, kb, 0:129], st_=st_: e.matmul(o, lhsT=lt, rhs=r_, start=st_, stop=False, skip_group_check=True),
                       [RPT[pi], Rkv[kb]], [Rbank[ab]])
            smi = nxt(small_n, 8)
            sm, Rsm = small[smi], Rsmall[smi]
            for qs in range(nqs):
                ab = h % 2
                col = qs * 129
                V_(lambda e, o=sm[:rq[qs], qs:qs + 1], i=banks[ab][:rq[qs], col + 128:col + 129]: e.reciprocal(out=o, in_=i),
                   [Rbank[ab]], [Rsm])
            ti = nxt(tm_n, 2)
            t16, Rt = tm16[ti], Rtm16[ti]
            for qs in range(nqs):
                ab = h % 2
                col = qs * 129
                evac(t16[:rq[qs], qs * 128:(qs + 1) * 128], banks[ab][:rq[qs], col:col + 128], [Rbank[ab], Rsm], [Rt],
                     scale=sm[:rq[qs], qs:qs + 1])
            def fin_pe(h=h, t16=t16, Rt=Rt):
                tb = 6
                pb = bf(banks[tb])
                for qs in range(nqs):
                    transpose(pb[:, qs * 128:qs * 128 + rq[qs]], t16[:rq[qs], qs * 128:(qs + 1) * 128], rq[qs], [Rt], [Rbank[tb]])
                oi = h % 2
                evac(olatT[oi][:, 0:nt], pb[:, 0:nt], [Rbank[tb]], [RolatT[oi]])
                ub = 7
                T_(lambda e, o=banks[ub][:, 0:nt], lt=wuv[:, h, :], r_=olatT[oi][:, 0:nt], h=h:
                   e.matmul(o, lhsT=lt, rhs=r_, start=(h % 2 == 0), stop=(h % 2 == 1)), [Rlw, RolatT[oi]], [Rbank[ub]])
                if h % 2 == 1:
                    V_(lambda e, o=omlaT[:, h // 2, 0:nt], i=banks[ub][:, 0:nt], z=zmlaT[:, h // 2, 0:nt]:
                       e.tensor_tensor(out=o, in0=i, in1=z, op=ALU.mult), [Rbank[ub], RzmlaT], [RomlaT])

            pend[0] = fin_pe
            next(bgen, None)
            next(bgen, None)

        if pend[0] is not None:
            pend[0]()
            pend[0] = None
        for _ in bgen:
            pass

        for h in range(8):
            g = h // 2
            sub = 128 if h == 0 else 256
            started = [False]

            hh = h % 2

            def d_score(kb):
                ks = ks_of(kb)
                c0, c1 = col_range(kb)
                n = c1 - c0
                sbk = next_bank(2, 6)
                T_(lambda e, o=banks[sbk][:ks, 0:2 * n].rearrange("p (m t) -> p m t", m=2),
                   lt=KdT[:, g, kb * 128:kb * 128 + ks],
                   r_=qdT[:, g, 2 * hh * TT:(2 * hh + 2) * TT].rearrange("p (m t) -> p m t", m=2)[:, :, c0:c1]:
                   e.matmul(o, lhsT=lt, rhs=r_, start=True, stop=True),
                   [Rkv[kb], RqdT], [Rbank[sbk]])
                return sbk

            slope_h = 2.0 ** (-(h + 1))
            q0 = b0 * 128
            kbl = [kb for kb in range(nkb) if slope_h * (q0 - (kb * 128 + ks_of(kb) - 1)) <= 160.0]
            squeue = [d_score(k) for k in kbl[:2]]
            for ki, kb in enumerate(kbl):
                sbk = squeue.pop(0)
                if ki + 2 < len(kbl):
                    squeue.append(d_score(kbl[ki + 2]))
                ks = ks_of(kb)
                c0, c1 = col_range(kb)
                n = c1 - c0
                pi = nxt(pt_n, NPT)
                PTv = PT[pi][:ks, 0:2 * n].rearrange("p (m t) -> p m t", m=2)
                Sv = banks[sbk][:ks, 0:2 * n].rearrange("p (m t) -> p m t", m=2)
                k0 = (c0 // sub) * sub
                while k0 < c1:
                    a0, a1 = max(k0, c0), min(k0 + sub, c1)
                    dd = b0 + (min(k0 + sub, nt) - 1) // 128 - kb
                    A_(lambda e, o=PTv[:, :, a0 - c0:a1 - c0], i=Sv[:, :, a0 - c0:a1 - c0], bb=btab[:ks, h, dd:dd + 1]:
                       e.activation(out=o, in_=i, func=AF.Exp, bias=bb, scale=DSCALE), [Rbank[sbk], Rconst], [RPT[pi]])
                    k0 += sub
                if kb >= b0:
                    w = min(128, n)
                    V_(lambda e, o=PTv[:, :, 0:w], mk=masks[:ks, h, 0:w].unsqueeze(1).to_broadcast([ks, 2, w]):
                       e.tensor_tensor(out=o, in0=o, in1=mk, op=ALU.mult), [RPT[pi], Rconst], [RPT[pi]])
                for m in range(2):
                    for qs in range(max(0, kb - b0), nqs):
                        ab = h % 2
                        col = m * 256 + qs * 65
                        st_ = not started[0]
                        started[0] = True
                        T_(lambda e, o=banks[ab][:rq[qs], col:col + 65], lt=PTv[:, m, qs * 128 - c0:qs * 128 - c0 + rq[qs]],
                           r_=Vd[:ks, kb, h, 0:65], st_=st_: e.matmul(o, lhsT=lt, rhs=r_, start=st_, stop=False, skip_group_check=True),
                           [RPT[pi], Rkv[kb]], [Rbank[ab]])
            smi = nxt(small_n, 8)
            sm, Rsm = small[smi], Rsmall[smi]
            ab = h % 2
            for m in range(2):
                V_(lambda e, o=sm[:rq[0], 4 * m:4 * m + nqs], i=banks[ab][:rq[0], m * 256:m * 256 + 65 * nqs].rearrange("p (q c) -> p q c", c=65)[:, :, 64]:
                   e.reciprocal(out=o, in_=i), [Rbank[ab]], [Rsm])
            V_(lambda e, o=sm[:rq[0], 4:4 + nqs]: e.tensor_scalar(out=o, in0=o, scalar1=lamv[:rq[0], 0:1], scalar2=None, op0=ALU.mult),
               [Rsm, Rvec], [Rsm])
            si = nxt(scr_n, NSCR)
            s, Rs = scr[si], Rscr[si]
            for qs in range(nqs):
                V_(lambda e, o=s[:rq[qs], qs * 64:(qs + 1) * 64], i=banks[ab][:rq[qs], 256 + qs * 65:256 + qs * 65 + 64], sc=sm[:rq[qs], 4 + qs:5 + qs]:
                   e.tensor_scalar(out=o, in0=i, scalar1=sc, scalar2=None, op0=ALU.mult), [Rbank[ab], Rsm], [Rs])
                V_(lambda e, o=od_raw[:rq[qs], qs, h * 64:(h + 1) * 64], i=banks[ab][:rq[qs], qs * 65:qs * 65 + 64],
                   sc=sm[:rq[qs], qs:qs + 1], t=s[:rq[qs], qs * 64:(qs + 1) * 64]:
                   e.scalar_tensor_tensor(out=o, in0=i, scalar=sc, in1=t, op0=ALU.mult, op1=ALU.add), [Rbank[ab], Rsm, Rs], [Rodraw])
            next(bgen, None)
        def post():
            for qs in range(nqs):
                diff_post_qs(qs)

        def diff_post_qs(qs):
            r = rq[qs]
            s, Rs = scr[5], Rscr[5]
            smi = nxt(small_n, 8)
            sm, Rsm = small[smi], Rsmall[smi]
            V_(lambda e, o=s[:r, :], i=od_raw[:r, qs, :]: e.tensor_tensor(out=o, in0=i, in1=i, op=ALU.mult), [Rodraw], [Rs])
            V_(lambda e, o=sm[:r, 0:8], i=s[:r, :].rearrange("p (h v) -> p h v", h=8): e.tensor_reduce(out=o, in_=i, axis=AX.X, op=ALU.add),
               [Rs], [Rsm])
            rstd_from_ssq(sm[:r, 0:8], 64, Rsm)
            V_(lambda e, o=s[:r, :].rearrange("p (h v) -> p h v", h=8), i=od_raw[:r, qs, :].rearrange("p (h v) -> p h v", h=8),
               rb=sm[:r, 0:8].unsqueeze(2).to_broadcast([r, 8, 64]): e.tensor_tensor(out=o, in0=i, in1=rb, op=ALU.mult),
               [Rodraw, Rsm], [Rs])
            ti = nxt(tm_n, 2)
            t16, Rt = tm16[ti], Rtm16[ti]
            V_(lambda e, o=t16[:r, 0:512], i=s[:r, :], z=zdiff[:r, qs, :]: e.tensor_tensor(out=o, in0=i, in1=z, op=ALU.mult),
               [Rs, Rzdiff], [Rt])
            tb = next_bank(6, 8)
            pb = bf(banks[tb])
            for c in range(4):
                transpose(pb[:, c * 128:c * 128 + r], t16[:r, c * 128:(c + 1) * 128], r, [Rt], [Rbank[tb]])
            evac(odiffT[:, :, qs * 128:qs * 128 + r], pb[:, 0:512].rearrange("p (c t) -> p c t", c=4)[:, :, 0:r], [Rbank[tb]], [RodiffT])

        return post

    def tile_compute(l, sq, b0, nt, first_tile, last_tile, lam_init, xsrc):
        nb = (nt + 127) // 128
        rows = [min(128, nt - 128 * j) for j in range(nb)]
        tok0 = 0 if sq == 1 else b0 * 128

        ri = nxt(rope_n, 2)
        ropet, Rropet = ropeb[ri], Rropeb[ri]
        LD(ropet[:, :, 0:nb, :], rope_d[:, :, b0:b0 + nb, :], [], [Rropet])
        pA, RpA = load_panel("in", l, (0, 416), 0)
        pz, Rpz = load_panel("in", l, (C_ZM, 512), 2)

        for j in range(nb):
            r = rows[j]
            xi = nxt(xblk_n, 2)
            xb, Rxb = xblk[xi], Rxblk[xi]
            LD(xb[:r, :], xsrc[tok0 + 128 * j:tok0 + 128 * j + r, :], [Rxres[sq][b0 - (16 if sq else 0) + j]] if l > 0 else [], [Rxb])
            smi = nxt(small_n, 8)
            sm, Rsm = small[smi], Rsmall[smi]
            x16, Rx16 = xs16[xi], Rxs16[xi]
            A_(lambda e, i=xb[:r, :], a=sm[:r, 0:1], jk=x16[:r, :]: e.activation(out=jk, in_=i, func=AF.Square, accum_out=a),
               [Rxb], [Rsm, Rx16])
            rstd_from_ssq(sm[:r, 0:1], D, Rsm)
            V_(lambda e, o=x16[:r, :], i=xb[:r, :], s=sm[:r, 0:1]: e.tensor_scalar(out=o, in0=i, scalar1=s, scalar2=None, op0=ALU.mult),
               [Rxb, Rsm], [Rx16])
            b = next_bank()
            pb = bf(banks[b])
            for kc in range(8):
                transpose(pb[:, kc * 128:kc * 128 + r], x16[:r, kc * 128:(kc + 1) * 128], r, [Rx16], [Rbank[b]])
            for kc in range(8):
                evac(xnT[:, kc, 128 * j:128 * j + r], pb[:, kc * 128:kc * 128 + r], [Rbank[b], Rvec], [RxnT],
                     scale=vec[:, V_G + kc:V_G + kc + 1])

        def tm_proj(panel, pres, j, ncols, cofs=0):
            r = rows[j]
            b = next_bank()
            mm_chain(banks[b][:r, 0:ncols], [(xnT[:, kc, 128 * j:128 * j + r], panel[:, kc, cofs:cofs + ncols]) for kc in range(8)],
                     [RxnT, pres], [Rbank[b]])
            return b

        stgs = []
        bA_ = [tm_proj(pA, RpA, j, 416) for j in range(nb)]
        for j in range(nb):
            r = rows[j]
            b = bA_[j]
            si = nxt(stg_n, 2)
            st, Rst = stg[si], Rstg[si]
            stgs.append((st, Rst))
            smi = nxt(small_n, 8)
            sm, Rsm = small[smi], Rsmall[smi]
            ti = nxt(tm_n, 2)
            t16, Rt = tm16[ti], Rtm16[ti]
            A_(lambda e, i=banks[b][:r, 0:256], a=sm[:r, 0:1], jk=t16[:r, 0:256]: e.activation(out=jk, in_=i, func=AF.Square, accum_out=a),
               [Rbank[b]], [Rsm, Rt])
            A_(lambda e, i=banks[b][:r, 256:384], a=sm[:r, 1:2], jk=t16[:r, 256:384]: e.activation(out=jk, in_=i, func=AF.Square, accum_out=a),
               [Rbank[b]], [Rsm, Rt])
            rstd_from_ssq(sm[:r, 0:1], 256, Rsm)
            rstd_from_ssq(sm[:r, 1:2], 128, Rsm)
            V_(lambda e, o=t16[:r, 0:256], i=banks[b][:r, 0:256], s=sm[:r, 0:1]: e.tensor_scalar(out=o, in0=i, scalar1=s, scalar2=None, op0=ALU.mult),
               [Rbank[b], Rsm], [Rt])
            V_(lambda e, o=st[:r, 0:128], i=banks[b][:r, 256:384], s=sm[:r, 1:2], gg=vec[:r, V_KVG:V_KVG + 128]:
               e.scalar_tensor_tensor(out=o, in0=i, scalar=s, in1=gg, op0=ALU.mult, op1=ALU.mult), [Rbank[b], Rsm, Rvec], [Rst])
            rope_tm(st[:r, 128:160].rearrange("p (h w) -> p h w", h=1), banks[b][:r, 384:416].rearrange("p (h w) -> p h w", h=1),
                    r, (ropet, Rropet), 1, 32, j, Rbank[b], Rst)
            tb = next_bank()
            pb = bf(banks[tb])
            for kc in range(2):
                transpose(pb[:, kc * 128:kc * 128 + r], t16[:r, kc * 128:(kc + 1) * 128], r, [Rt], [Rbank[tb]])
            for kc in range(2):
                evac(cqnT[:, kc, 128 * j:128 * j + r], pb[:, kc * 128:kc * 128 + r], [Rbank[tb], Rvec], [RcqnT],
                     scale=vec[:, V_QG + kc:V_QG + kc + 1])
        for c in range(4):
            b = next_bank()
            mm_chain(banks[b][:, 0:nt], [(pz[:, kc, c * 128:(c + 1) * 128], xnT[:, kc, 0:nt]) for kc in range(8)], [RxnT, Rpz], [Rbank[b]])
            A_(lambda e, o=zmlaT[:, c, 0:nt], i=banks[b][:, 0:nt]: e.activation(out=o, in_=i, func=AF.Silu), [Rbank[b]], [RzmlaT])
        pq, Rpq = load_panel("in", l, (C_QD, 512), 3)
        bq_ = [tm_proj(pq, Rpq, j, 512) for j in range(nb)]
        for j in range(nb):
            r = rows[j]
            b = bq_[j]
            ti = nxt(tm_n, 2)
            t16, Rt = tm16[ti], Rtm16[ti]
            evac(t16[:r, 0:512], banks[b][:r, 0:512], [Rbank[b]], [Rt])
            tb = next_bank()
            pb = bf(banks[tb])
            for g in range(4):
                transpose(pb[:, g * 128:g * 128 + r], t16[:r, g * 128:(g + 1) * 128], r, [Rt], [Rbank[tb]])
            for pgp in range(4):
                evac(qdT[32 * pgp:32 * pgp + 32, :, pgp * TT + 128 * j:pgp * TT + 128 * j + r],
                     pb[32 * pgp:32 * pgp + 32, 0:512].rearrange("p (g t) -> p g t", g=4)[:, :, 0:r], [Rbank[tb]], [RqdT])
        pk, Rpk = load_panel("in", l, (C_KD, 512), 1)
        for j in range(nb):
            r = rows[j]
            b = tm_proj(pk, Rpk, j, 512)
            evac(stgs[j][0][:r, 160:672], banks[b][:r, 0:512], [Rbank[b]], [stgs[j][1]])
        pv, Rpv = load_panel("in", l, (C_VD, 512), 0)
        for j in range(nb):
            r = rows[j]
            b = tm_proj(pv, Rpv, j, 512)
            st, Rst = stgs[j]
            evac(st[:r, 672:1184], banks[b][:r, 0:512], [Rbank[b]], [Rst])
            t0 = tok0 + 128 * j
            ST(o_ckv[sq][l, t0:t0 + r, :], st[:r, 0:128], [Rst], [])
            ST(o_kr[sq][l, t0:t0 + r, :], st[:r, 128:160], [Rst], [])
            ST(o_dk[sq][l, t0:t0 + r, :], st[:r, 160:672], [Rst], [])
            ST(o_dv[sq][l, t0:t0 + r, :], st[:r, 672:1184], [Rst], [])
            kv_build_block(b0 + j, r, st, Rst)
        pzd, Rpzd = load_panel("in", l, (C_ZD, 512), 2)
        for j in range(nb):
            r = rows[j]
            b = tm_proj(pzd, Rpzd, j, 512)
            si = nxt(scr_n, NSCR)
            s, Rs = scr[si], Rscr[si]
            A_(lambda e, o=s[:r, :], i=banks[b][:r, 0:512]: e.activation(out=o, in_=i, func=AF.Silu), [Rbank[b]], [Rs])
            V_(lambda e, o=zdiff[:r, j, :], i=s[:r, :], gg=vec[:r, V_SUB:V_SUB + 512], li=lam_init:
               e.scalar_tensor_tensor(out=o, in0=i, scalar=1.0 - li, in1=gg, op0=ALU.mult, op1=ALU.mult), [Rs, Rvec], [Rzdiff])
        bqm = []
        for j in range(nb):
            r = rows[j]
            bA = next_bank()
            bB = next_bank()
            mm_chain(banks[bA][:r, 0:480], [(cqnT[:, kc, 128 * j:128 * j + r], wuq[:, kc, 0:480]) for kc in range(2)], [RcqnT, Rlw], [Rbank[bA]])
            mm_chain(banks[bB][:r, 0:288], [(cqnT[:, kc, 128 * j:128 * j + r], wuq[:, kc, 480:768]) for kc in range(2)], [RcqnT, Rlw], [Rbank[bB]])
            bqm.append((bA, bB))
        for j in range(nb):
            r = rows[j]
            bA, bB = bqm[j]
            ti = nxt(tm_n, 2)
            t16, Rt = tm16[ti], Rtm16[ti]
            for (bb, h0, nh) in ((bA, 0, 5), (bB, 5, 3)):
                src = banks[bb][:r, 0:nh * 96].rearrange("p (h w) -> p h w", h=nh)
                dst = t16[:r, h0 * 96:(h0 + nh) * 96].rearrange("p (h w) -> p h w", h=nh)
                evac(dst[:, :, 0:64], src[:, :, 0:64], [Rbank[bb]], [Rt])
                rope_tm(dst, src, r, (ropet, Rropet), nh, 96, j, Rbank[bb], Rt)
            tb = next_bank()
            pb = bf(banks[tb])
            for h in range(8):
                transpose(pb[0:96, h * 128:h * 128 + r], t16[:r, h * 96:(h + 1) * 96], r, [Rt], [Rbank[tb]])
            evac(qT[0:96, :, 128 * j:128 * j + r], pb[0:96, :].rearrange("p (h t) -> p h t", h=8)[:, :, 0:r], [Rbank[tb]], [RqT])
        if first_tile:
            if sq == 0:
                V_(lambda e: e.memset(xl[:, :, 0:3], 0.0), [], [Rxl])
                V_(lambda e: e.memset(hcarry[:, :], 0.0), [], [Rhc])
            else:
                V_(lambda e: e.tensor_copy(out=xl[:, :, 0:3], in_=sv[:, 4:16].rearrange("p (c k) -> p c k", c=4)), [Rvec], [Rxl])
                V_(lambda e: e.tensor_copy(out=hcarry[:, :], in_=sv[:, 0:4]), [Rvec], [Rhc])
        pxl, Rpxl = load_panel("in", l, (C_XL, 512), 3)
        for c in range(4):
            b = next_bank()
            mm_chain(banks[b][:, 0:nt], [(pxl[:, kc, c * 128:(c + 1) * 128], xnT[:, kc, 0:nt]) for kc in range(8)], [RxnT, Rpxl], [Rbank[b]])
            evac(xl[:, c, 3:3 + nt], banks[b][:, 0:nt], [Rbank[b]], [Rxl])
        pzl, Rpzl = load_panel("in", l, (C_ZL, 512), 1)
        for c in range(4):
            b = next_bank()
            mm_chain(banks[b][:, 0:nt], [(pzl[:, kc, c * 128:(c + 1) * 128], xnT[:, kc, 0:nt]) for kc in range(8)], [RxnT, Rpzl], [Rbank[b]])
            A_(lambda e, o=zlruT[:, c, 0:nt], i=banks[b][:, 0:nt]: e.activation(out=o, in_=i, func=AF.Silu), [Rbank[b]], [RzlruT])

        def lru_gen():
            mo = V_MISC

            def hb(i):
                if i < 12:
                    return scr[i // 2][:, (i % 2) * 256:(i % 2) * 256 + nt], Rscr[i // 2]
                j = i - 12
                return od_raw[:, j // 2, (j % 2) * 256:(j % 2) * 256 + nt], Rodraw

            XC = [hb(4 * c) for c in range(4)]
            AA = [hb(4 * c + 1) for c in range(4)]
            BB = [hb(4 * c + 2) for c in range(4)]
            TTb = [hb(4 * c + 3) for c in range(4)]
            for c in range(4):
                xc, Rxc = XC[c]
                V_(lambda e, o=xc, i=xl[:, c, 0:nt], w=vec[:, V_CW + 4 * c:V_CW + 4 * c + 1], bb=vec[:, mo + c:mo + c + 1]:
                   e.tensor_scalar(out=o, in0=i, scalar1=w, scalar2=bb, op0=ALU.mult, op1=ALU.add), [Rxl, Rvec], [Rxc])
                for k in range(1, 4):
                    V_(lambda e, o=xc, i=xl[:, c, k:k + nt], w=vec[:, V_CW + 4 * c + k:V_CW + 4 * c + k + 1]:
                       e.scalar_tensor_tensor(out=o, in0=i, scalar=w, in1=o, op0=ALU.mult, op1=ALU.add), [Rxl, Rvec, Rxc], [Rxc])
                xc16, Rxc16 = PT[c], RPT[c]
                V_(lambda e, o=xc16[:, 0:nt], i=xc: e.tensor_copy(out=o, in_=i), [Rxc], [Rxc16])
                bk = 6
                T_(lambda e, o=banks[bk][:, 0:nt], lt=bda[:, c, :], r_=xc16[:, 0:nt]: e.matmul(o, lhsT=lt, rhs=r_, start=True, stop=True),
                   [Rlw, Rxc16], [Rbank[bk]])
                T_(lambda e, o=banks[bk][:, 256:256 + nt], lt=bdx[:, c, :], r_=xc16[:, 0:nt]: e.matmul(o, lhsT=lt, rhs=r_, start=True, stop=True),
                   [Rlw, Rxc16], [Rbank[bk]])
                A_(lambda e, o=AA[c][0], i=banks[bk][:, 0:nt], bb=vec[:, mo + 4 + c:mo + 5 + c]: e.activation(out=o, in_=i, func=AF.Sigmoid, bias=bb),
                   [Rbank[bk], Rvec], [AA[c][1]])
                A_(lambda e, o=BB[c][0], i=banks[bk][:, 256:256 + nt], bb=vec[:, mo + 8 + c:mo + 9 + c]: e.activation(out=o, in_=i, func=AF.Sigmoid, bias=bb),
                   [Rbank[bk], Rvec], [BB[c][1]])
                yield
            for c in range(4):
                A_(lambda e, o=TTb[c][0], i=AA[c][0], s_=lrv[:, 4 + c:5 + c]: e.activation(out=o, in_=i, func=AF.Exp, scale=s_), [AA[c][1], Rvec], [TTb[c][1]])
                A_(lambda e, o=AA[c][0], s_=lrv[:, c:c + 1]: e.activation(out=o, in_=o, func=AF.Exp, scale=s_), [AA[c][1], Rvec], [AA[c][1]])
                if c % 2 == 1:
                    yield
            for c in range(4):
                A_(lambda e, o=TTb[c][0]: e.activation(out=o, in_=o, func=AF.Ln, scale=-1.0, bias=1.0), [TTb[c][1]], [TTb[c][1]])
            for c in range(4):
                A_(lambda e, o=TTb[c][0]: e.activation(out=o, in_=o, func=AF.Exp, scale=0.5), [TTb[c][1]], [TTb[c][1]])
            yield
            for c in range(4):
                t_, Rt_ = TTb[c]
                b_, Rb = BB[c]
                a_, Ra = AA[c]
                xc, Rxc = XC[c]
                if first_tile and sq == 0:
                    V_(lambda e, o=t_[:, 0:1]: e.memset(o, 1.0), [Rt_], [Rt_])
                V_(lambda e, o=b_, i=t_: e.tensor_tensor(out=o, in0=o, in1=i, op=ALU.mult), [Rb, Rt_], [Rb])
                V_(lambda e, o=b_, i=xc: e.tensor_tensor(out=o, in0=o, in1=i, op=ALU.mult), [Rb, Rxc], [Rb])
                V_(lambda e, o=t_, d0=a_, d1=b_, ini=hcarry[:, c:c + 1]:
                   e.tensor_tensor_scan(out=o, data0=d0, data1=d1, initial=ini, op0=ALU.mult, op1=ALU.add), [Ra, Rb, Rhc], [Rt_])
                V_(lambda e, o=hcarry[:, c:c + 1], i=t_[:, nt - 1:nt]: e.tensor_copy(out=o, in_=i), [Rt_], [Rhc])
                V_(lambda e, o=olruT[:, c, 0:nt], i=t_, z=zlruT[:, c, 0:nt]: e.tensor_tensor(out=o, in0=i, in1=z, op=ALU.mult),
                   [Rt_, RzlruT], [RolruT])
                yield
            if last_tile:
                for c in range(4):
                    ST(o_conv[sq][l][:, c * 128:(c + 1) * 128].rearrange("k p -> p k"), xl[:, c, nt:nt + 3], [Rxl], [], slow=True)
                ST(o_h[sq][l].rearrange("(c p) -> p c", p=128), hcarry[:, :], [Rhc], [], slow=True)
            else:
                V_(lambda e: e.tensor_copy(out=xl[:, :, 0:3], in_=xl[:, :, nt:nt + 3]), [Rxl], [Rxl])

            yield

        XS = (0, 3, 0)
        pX = {0: load_panel("o", l, 0, XS[0])}
        pY = {(0, 0): load_panel("in", l, (C_G, 512), 1), (0, 1): load_panel("in", l, (C_G + 512, 512), 2)}
        pX[1] = load_panel("o", l, 1, XS[1])

        bgen = lru_gen()
        diff_post = attention_tile(l, b0, nt, lam_init, bgen)
        for _ in bgen:
            pass

        xres_blk = []
        for j in range(nb):
            r = rows[j]
            xi = nxt(xblk_n, 2)
            xb, Rxb = xblk[xi], Rxblk[xi]
            LD(xb[:r, :], xsrc[tok0 + 128 * j:tok0 + 128 * j + r, :], [Rxres[sq][b0 - (16 if sq else 0) + j]] if l > 0 else [], [Rxb])
            xres_blk.append((xb, Rxb))

        mergedT, RmT = qT, RqT
        obT = ((omlaT, RomlaT), (odiffT, RodiffT), (olruT, RolruT))

        def hbw(i):
            return scr[i // 2][:, (i % 2) * 256:(i % 2) * 256 + nt], Rscr[i // 2]

        pw = {}
        for br_ in range(3):
            if br_ == 1:
                diff_post()
            po, Rpo = pX[br_]
            oT, RoT = obT[br_]
            for half in range(2):
                pg, Rpg = pY[(br_, half)]
                for e4 in range(4):
                    ec = half * 4 + e4
                    bg = next_bank()
                    bo = next_bank()
                    mm_chain(banks[bg][:, 0:nt], [(pg[:, kc, e4 * 128:(e4 + 1) * 128], xnT[:, kc, 0:nt]) for kc in range(8)],
                             [RxnT, Rpg], [Rbank[bg]])
                    mm_chain(banks[bo][:, 0:nt], [(po[:, wc, ec * 128:(ec + 1) * 128], oT[:, wc, 0:nt]) for wc in range(4)],
                             [RoT, Rpo], [Rbank[bo]])
                    gs, Rg = hbw(8 + nxt(g_n, 2))
                    A_(lambda e, o=gs, i=banks[bg][:, 0:nt]: e.activation(out=o, in_=i, func=AF.Sigmoid), [Rbank[bg]], [Rg])
                    ma, Rma = hbw(ec)
                    if br_ == 0:
                        V_(lambda e, o=ma, g_=gs, i=banks[bo][:, 0:nt]: e.tensor_tensor(out=o, in0=g_, in1=i, op=ALU.mult),
                           [Rg, Rbank[bo]], [Rma])
                    else:
                        V_(lambda e, g_=gs, i=banks[bo][:, 0:nt]: e.tensor_tensor(out=g_, in0=g_, in1=i, op=ALU.mult),
                           [Rg, Rbank[bo]], [Rg])
                        if br_ == 1:
                            V_(lambda e, o=ma, g_=gs: e.tensor_tensor(out=o, in0=o, in1=g_, op=ALU.add), [Rg, Rma], [Rma])
                        else:
                            V_(lambda e, o=mergedT[:, ec, 0:nt], m_=ma, g_=gs: e.tensor_tensor(out=o, in0=m_, in1=g_, op=ALU.add),
                               [Rg, Rma], [RmT])
                if half == 0:
                    if br_ + 1 <= 2:
                        pY[(br_ + 1, 0)] = load_panel("in", l, (C_G + (br_ + 1) * 1024, 512), 1)
                    else:
                        pw[1] = load_panel("out", l, 1, 1)
                else:
                    if br_ + 1 <= 2:
                        pY[(br_ + 1, 1)] = load_panel("in", l, (C_G + (br_ + 1) * 1024 + 512, 512), 2)
                    if br_ + 2 <= 2:
                        pX[br_ + 2] = load_panel("o", l, br_ + 2, XS[br_ + 2])
                    elif br_ == 1:
                        pw[0] = load_panel("out", l, 0, 3)
        for j in range(nb):
            r = rows[j]
            xb, Rxb = xres_blk[j]
            for eh in range(2):
                b = next_bank()
                mm_chain(banks[b][:r, :], [(mergedT[:, dc, 128 * j:128 * j + r], pw[eh][0][:, dc, :]) for dc in range(8)],
                         [RmT, pw[eh][1]], [Rbank[b]])
                V_(lambda e, o=xb[:r, eh * 512:(eh + 1) * 512], i=banks[b][:r, :]: e.tensor_tensor(out=o, in0=o, in1=i, op=ALU.add),
                   [Rbank[b], Rxb], [Rxb])
            t0 = tok0 + 128 * j
            rx = Rxres[sq][b0 - (16 if sq else 0) + j]
            if l < depth - 1:
                ST(xres[sq][t0:t0 + r, :], xb[:r, :], [Rxb], [rx])
            else:
                smi = nxt(small_n, 8)
                sm, Rsm = small[smi], Rsmall[smi]
                A_(lambda e, i=xb[:r, :], a=sm[:r, 0:1], jk=xs16[0][:r, :]: e.activation(out=jk, in_=i, func=AF.Square, accum_out=a),
                   [Rxb], [Rsm, Rxs16[0]])
                rstd_from_ssq(sm[:r, 0:1], D, Rsm)
                V_(lambda e, o=xb[:r, :], s=sm[:r, 0:1], gg=fnorm_t[:r, :]: e.scalar_tensor_tensor(out=o, in0=o, scalar=s, in1=gg, op0=ALU.mult, op1=ALU.mult),
                   [Rxb, Rsm, Rconst], [Rxb])
                ST((y_s if sq else y_p)[t0:t0 + r, :], xb[:r, :], [Rxb], [])

    Rxres = {0: [P.res(f"xrp{b}") for b in range(seq // 128)], 1: [P.res("xrs")]}

    for l in range(depth):
        if l + 1 < depth:
            cast_layer_weights(l + 1)
        lam_init = layer_setup(l)
        ntiles = seq // TT
        for i in range(ntiles):
            tile_compute(l, 0, i * (TT // 128), TT, i == 0, i == ntiles - 1, lam_init, x_p if l == 0 else xres[0])
        if do_sample:
            for kb in range(PAST // 128):
                si = nxt(stg_n, 2)
                st, Rst = stg[si], Rstg[si]
                LD(st[:, 0:128], c_ckv[l, kb * 128:(kb + 1) * 128, :], [], [Rst])
                LD(st[:, 128:160], c_kr[l, kb * 128:(kb + 1) * 128, :], [], [Rst])
                LD(st[:, 160:672], c_dk[l, kb * 128:(kb + 1) * 128, :], [], [Rst])
                LD(st[:, 672:1184], c_dv[l, kb * 128:(kb + 1) * 128, :], [], [Rst])
                kv_build_block(kb, 128, st, Rst)
            tile_compute(l, 1, PAST // 128, DEC_SEQ, True, True, lam_init, x_s if l == 0 else xres[1])
    stats = P.emit()
    return nc, stats


def _consts():
    half = 16
    inv = (10000.0 ** (-np.arange(half, dtype=np.float32) / half)).astype(np.float32)
    pos = (np.arange(32)[None, :] * 128 + np.arange(128)[:, None]).astype(np.float32)
    ang = pos[:, :, None] * inv[None, None, :]
    rope = np.zeros((128, 2, 32, 16), np.float32)
    rope[:, 0] = np.cos(ang)
    rope[:, 1] = np.sin(ang)
    slopes = 2.0 ** (-8.0 * np.arange(1, 9, dtype=np.float64) / 8)
    p = np.arange(128, dtype=np.float64)
    d = np.arange(33, dtype=np.float64)
    btab = -(slopes[None, :, None] * (127.0 - p[:, None, None] + 128.0 * d[None, None, :]))
    s = np.arange(128)[:, None]
    t = np.arange(128)[None, :]
    vis = (s // 64) <= (t // 64)
    masks = np.zeros((128, 9, 128), np.float64)
    for h in range(8):
        m = np.where(s > t, np.exp(-2.0 * slopes[h] * (s - t)), 1.0)
        masks[:, h, :] = np.where(vis, m, 0.0)
    masks[:, 8, :] = vis.astype(np.float64)
    return rope, btab.astype(np.float32), masks.astype(np.float32), np.eye(128, dtype=np.float32)


def _col(v, n):
    return np.ascontiguousarray(v.reshape(n, 128).T)


def _panels_in(w):
    L = w.shape[0]
    wk = w.reshape(L, 8, 128, NIN).transpose(0, 2, 1, 3)
    out = np.zeros((L, 14, 128, 4096), np.float32)
    out[:, 0, :, 0:8 * 416] = wk[:, :, :, 0:416].reshape(L, 128, 8 * 416)
    for p in range(13):
        out[:, 1 + p] = wk[:, :, :, 416 + 512 * p:416 + 512 * (p + 1)].reshape(L, 128, 4096)
    return out


def _pad_uv(w):
    L = w.shape[0]
    out = np.zeros((L, 128, 8, 128), np.float32)
    for h in range(8):
        out[:, :, h, 64 * (h % 2):64 * (h % 2) + 64] = w[:, :, h, :]
    return out


def _prep_shared(inp):
    L = DEPTH
    rope, btab, masks, ident = _consts()
    vecs = np.zeros((L, 128, NV), np.float32)
    for l in range(L):
        vecs[l, :, V_G:V_G + 8] = _col(inp["norm_g"][l], 8)
        vecs[l, :, V_QG:V_QG + 2] = _col(inp["mla_q_norm"][l], 2)
        vecs[l, :, V_KVG:V_KVG + 128] = inp["mla_kv_norm"][l][None, :]
        vecs[l, :, V_SUB:V_SUB + 512] = np.tile(inp["diff_subln"][l], 8)[None, :]
        for i, k in enumerate(("diff_lq1", "diff_lk1", "diff_lq2", "diff_lk2")):
            vecs[l, :, V_LAM + 32 * i:V_LAM + 32 * (i + 1)] = inp[k][l][None, :]
        cw = inp["lru_conv_w"][l]
        for c in range(4):
            for k in range(4):
                vecs[l, :, V_CW + 4 * c + k] = cw[k, c * 128:(c + 1) * 128]
        vecs[l, :, V_MISC + 0:V_MISC + 4] = _col(inp["lru_conv_b"][l], 4)
        vecs[l, :, V_MISC + 4:V_MISC + 8] = _col(inp["lru_b_a"][l], 4)
        vecs[l, :, V_MISC + 8:V_MISC + 12] = _col(inp["lru_b_x"][l], 4)
        vecs[l, :, V_MISC + 12:V_MISC + 16] = _col(inp["lru_lambda"][l], 4)

    def bd(w):
        out = np.zeros((L, 128, 4, 128), np.float32)
        for c in range(4):
            for nl in range(2):
                out[:, nl * 64:(nl + 1) * 64, c, nl * 64:(nl + 1) * 64] = w[:, 2 * c + nl]
        return out

    sh = {
        "vecs": vecs,
        "fnorm": np.ascontiguousarray(np.broadcast_to(inp["final_norm"][None, :], (128, D))).astype(np.float32),
        "w_in": _panels_in(inp["w_in"]),
        "w_o": np.ascontiguousarray(np.stack([inp["w_o_mla"], inp["w_o_diff"], inp["w_o_lru"]], 1)
                                    .reshape(L, 3, 4, 128, D).transpose(0, 1, 3, 2, 4)).reshape(L, 3, 128, 4096),
        "w_out": np.ascontiguousarray(inp["w_out"].reshape(L, 8, 128, 2, 512).transpose(0, 3, 2, 1, 4)).reshape(L, 2, 128, 4096),
        "w_uq": np.ascontiguousarray(inp["mla_w_uq"].reshape(L, 2, 128, 768).transpose(0, 2, 1, 3)),
        "w_ukT": np.ascontiguousarray(inp["mla_w_uk"].transpose(0, 3, 2, 1)),
        "w_uv": _pad_uv(inp["mla_w_uv"]),
        "bd_a": bd(inp["lru_w_a"]),
        "bd_x": bd(inp["lru_w_x"]),
        "ident": ident, "rope": rope, "btab": btab, "masks": masks,
    }
    return sh


_CACHE = {}


def kernel(**inputs):
    inp = {k: np.asarray(v) for k, v in inputs.items()}
    if "nc" not in _CACHE:
        _CACHE["nc"] = build_program()[0]
    nc = _CACHE["nc"]
    sh = _prep_shared(inp)
    in_maps = []
    for c in range(8):
        m = dict(sh)
        m["x_p"] = np.ascontiguousarray(inp["x_prompt"][c % 4])
        m["x_s"] = np.ascontiguousarray(inp["x_sample"][c])
        m["c_ckv"] = np.ascontiguousarray(inp["cache_mla_ckv"][:, c])
        m["c_kr"] = np.ascontiguousarray(inp["cache_mla_krope"][:, c])
        m["c_dk"] = np.ascontiguousarray(inp["cache_diff_k"][:, c].reshape(DEPTH, PAST, 512))
        m["c_dv"] = np.ascontiguousarray(inp["cache_diff_v"][:, c].reshape(DEPTH, PAST, 512))
        sv = np.zeros((DEPTH, 128, 16), np.float32)
        for l in range(DEPTH):
            sv[l, :, 0:4] = _col(inp["state_lru_h"][l, c], 4)
            cv = inp["state_lru_conv"][l, c]
            for ch in range(4):
                for k in range(3):
                    sv[l, :, 4 + 3 * ch + k] = cv[k, ch * 128:(ch + 1) * 128]
        m["svec"] = sv
        in_maps.append(m)
    res = run_bass_kernel_spmd(nc, in_maps, core_ids=list(range(8)))
    R = res.results
    B = 4
    y_prompt = np.stack([R[b]["y_p"] for b in range(B)], 0)
    y_sample = np.stack([R[c]["y_s"] for c in range(8)], 0)

    def pst(key, tail):
        return np.stack([R[b][key] for b in range(B)], 1).reshape((DEPTH, B) + tail)

    def sst(key, tail):
        return np.stack([R[c][key] for c in range(8)], 1).reshape((DEPTH, 8) + tail)

    outs = (y_prompt, y_sample,
            pst("p_ckv", (SEQ, 128)), pst("p_kr", (SEQ, 32)), pst("p_dk", (SEQ, 8, 64)), pst("p_dv", (SEQ, 8, 64)),
            pst("p_h", (512,)), pst("p_conv", (3, 512)),
            sst("s_ckv", (DEC_SEQ, 128)), sst("s_kr", (DEC_SEQ, 32)), sst("s_dk", (DEC_SEQ, 8, 64)), sst("s_dv", (DEC_SEQ, 8, 64)),
            sst("s_h", (512,)), sst("s_conv", (3, 512)))
    return tuple(np.ascontiguousarray(o, dtype=np.float32) for o in outs)
```

```python
import contextlib
import math
import numpy as np
import concourse.bass as bass
import concourse.mybir as mybir
from concourse.bass_utils import run_bass_kernel_spmd

F32 = mybir.dt.float32
BF16 = mybir.dt.bfloat16
AF = mybir.ActivationFunctionType
ALU = mybir.AluOpType
AX = mybir.AxisListType

D = 1024
SEQ = 4096
DEPTH = 4
DEC_SEQ = 16
PAST = 2048
NIN = 7072
EPS = 1e-6
TT = 256
QSCALE = 96 ** -0.5
DSCALE = 32 ** -0.5
C_CQ, C_CKV, C_KR, C_ZM, C_QD, C_KD, C_VD, C_ZD, C_XL, C_ZL, C_G = (
    0, 256, 384, 416, 928, 1440, 1952, 2464, 2976, 3488, 4000)
NV = 8 + 2 + 128 + 512 + 128 + 16 + 16
V_G, V_QG, V_KVG, V_SUB, V_LAM, V_CW, V_MISC = 0, 8, 10, 138, 650, 778, 794

ENGS = ("pe", "act", "dve", "pool", "sp")
NDMA_SEMS = 24


class Res:
    __slots__ = ("name", "last_w", "readers", "dma_readers", "excl")

    def __init__(self, name, excl=False):
        self.name = name
        self.last_w = None
        self.readers = {}
        self.dma_readers = []
        self.excl = excl


class Op:
    __slots__ = ("eng", "fn", "deps", "idx", "is_dma", "dma_n", "signal", "sigcount")

    def __init__(self, eng, fn, is_dma):
        self.eng = eng
        self.fn = fn
        self.deps = set()
        self.is_dma = is_dma
        self.signal = False
        self.sigcount = 0
        self.dma_n = -1


class Prog:
    def __init__(self, nc):
        self.nc = nc
        self.ops = {e: [] for e in ENGS}

    def res(self, name="", excl=False):
        return Res(name, excl)

    def op(self, eng, fn, reads=(), writes=(), dma=False):
        o = Op(eng, fn, dma)
        o.idx = len(self.ops[eng])
        reads = list(reads)
        writes = list(writes)
        for r in list(reads):
            if r.excl and r not in writes:
                writes.append(r)
        for r in reads:
            if r.last_w is not None:
                o.deps.add(r.last_w)
        for w in writes:
            if w.last_w is not None:
                o.deps.add(w.last_w)
            for rd in w.readers.values():
                o.deps.add(rd)
            for rd in w.dma_readers:
                o.deps.add(rd)
        for r in reads:
            if r in writes:
                continue
            if dma:
                r.dma_readers.append(o)
                if len(r.dma_readers) > 3 * NDMA_SEMS:
                    r.dma_readers = r.dma_readers[-2 * NDMA_SEMS:]
            else:
                r.readers[eng] = o
        for w in writes:
            w.last_w = o
            w.readers = {}
            w.dma_readers = []
        o.deps.discard(o)
        self.ops[eng].append(o)
        return o

    def emit(self):
        nc = self.nc
        dma_count = {e: 0 for e in ENGS}
        for e in ENGS:
            for o in self.ops[e]:
                if o.is_dma:
                    o.dma_n = dma_count[e]
                    dma_count[e] += 1
        for e in ENGS:
            for o in self.ops[e]:
                for d in o.deps:
                    if d.is_dma:
                        continue
                    if d.eng == o.eng and d.eng == "pe" and not o.is_dma:
                        continue
                    d.signal = True
        for e in ENGS:
            c = 0
            for o in self.ops[e]:
                if o.signal and not o.is_dma:
                    c += 1
                o.sigcount = c
        stack = contextlib.ExitStack()
        sems = {e: stack.enter_context(nc.semaphore(f"s_{e}")) for e in ENGS}
        dsems = {e: [stack.enter_context(nc.semaphore(f"d_{e}{i}")) for i in range(NDMA_SEMS)]
                 for e in ENGS if dma_count[e] > 0}
        block = stack.enter_context(nc.Block())
        engobj = {"pe": "tensor", "act": "scalar", "dve": "vector", "pool": "gpsimd", "sp": "sync"}
        stats = {}

        def make_body(e):
            def body(eng):
                seen = {x: 0 for x in ENGS}
                seen_dma = {}
                nwait = 0
                ops = self.ops[e]
                ndma = dma_count[e]
                for o in ops:
                    need = {}
                    need_dma = {}
                    for d in o.deps:
                        if d.is_dma:
                            key = (d.eng, d.dma_n % NDMA_SEMS)
                            val = 16 * (d.dma_n // NDMA_SEMS + 1)
                            if seen_dma.get(key, 0) < val:
                                need_dma[key] = max(need_dma.get(key, 0), val)
                        else:
                            if d.eng == e and e == "pe" and not o.is_dma:
                                continue
                            if seen[d.eng] < d.sigcount:
                                need[d.eng] = max(need.get(d.eng, 0), d.sigcount)
                    if o.is_dma and o.dma_n >= NDMA_SEMS:
                        key = (e, o.dma_n % NDMA_SEMS)
                        val = 16 * (o.dma_n // NDMA_SEMS)
                        if seen_dma.get(key, 0) < val:
                            need_dma[key] = max(need_dma.get(key, 0), val)
                    for x, v in need.items():
                        eng.wait_ge(sems[x], v)
                        seen[x] = v
                        nwait += 1
                    for key, v in need_dma.items():
                        eng.wait_ge(dsems[key[0]][key[1]], v)
                        seen_dma[key] = v
                        nwait += 1
                    ins = o.fn(eng)
                    if o.is_dma:
                        ins.then_inc(dsems[e][o.dma_n % NDMA_SEMS], 16)
                    elif o.signal:
                        ins.then_inc(sems[e], 1)
                for slot in range(min(ndma, NDMA_SEMS)):
                    last = ((ndma - 1 - slot) // NDMA_SEMS) * NDMA_SEMS + slot
                    val = 16 * (last // NDMA_SEMS + 1)
                    if seen_dma.get((e, slot), 0) < val:
                        eng.wait_ge(dsems[e][slot], val)
                stats[e] = (len(ops), nwait)
            return body

        for e in ENGS:
            if self.ops[e]:
                getattr(block, engobj[e])(make_body(e))
        stack.close()
        return stats


def build_program(depth=DEPTH, seq=SEQ, do_sample=True):
    nc = bass.Bass("TRN2", target_bir_lowering=False, dynamic_dma_scratch_size=4096)
    P = Prog(nc)
    NBLK = max(seq, PAST + 128 if do_sample else 0) // 128
    SCAP = NBLK * 128

    def din(name, shape):
        return nc.dram_tensor(name, list(shape), F32, kind="ExternalInput").ap()

    def dout(name, shape):
        return nc.dram_tensor(name, list(shape), F32, kind="ExternalOutput").ap()

    x_p = din("x_p", [seq, D])
    x_s = din("x_s", [DEC_SEQ, D])
    c_ckv = din("c_ckv", [DEPTH, PAST, 128])
    c_kr = din("c_kr", [DEPTH, PAST, 32])
    c_dk = din("c_dk", [DEPTH, PAST, 512])
    c_dv = din("c_dv", [DEPTH, PAST, 512])
    svec = din("svec", [DEPTH, 128, 16])
    vecs = din("vecs", [DEPTH, 128, NV])
    fnorm = din("fnorm", [128, D])
    w_in = din("w_in", [DEPTH, 14, 128, 4096])
    w_o = din("w_o", [DEPTH, 3, 128, 4096])
    w_out = din("w_out", [DEPTH, 2, 128, 4096])
    w_uq = din("w_uq", [DEPTH, 128, 2, 768])
    w_ukT = din("w_ukT", [DEPTH, 64, 8, 128])
    w_uv = din("w_uv", [DEPTH, 128, 8, 128])
    bd_a = din("bd_a", [DEPTH, 128, 4, 128])
    bd_x = din("bd_x", [DEPTH, 128, 4, 128])
    ident_d = din("ident", [128, 128])
    rope_d = din("rope", [128, 2, 32, 16])
    btab_d = din("btab", [128, 8, 33])
    mask_d = din("masks", [128, 9, 128])

    y_p = dout("y_p", [seq, D])
    y_s = dout("y_s", [DEC_SEQ, D])
    o_ckv = {0: dout("p_ckv", [DEPTH, seq, 128]), 1: dout("s_ckv", [DEPTH, DEC_SEQ, 128])}
    o_kr = {0: dout("p_kr", [DEPTH, seq, 32]), 1: dout("s_kr", [DEPTH, DEC_SEQ, 32])}
    o_dk = {0: dout("p_dk", [DEPTH, seq, 512]), 1: dout("s_dk", [DEPTH, DEC_SEQ, 512])}
    o_dv = {0: dout("p_dv", [DEPTH, seq, 512]), 1: dout("s_dv", [DEPTH, DEC_SEQ, 512])}
    o_h = {0: dout("p_h", [DEPTH, 512]), 1: dout("s_h", [DEPTH, 512])}
    o_conv = {0: dout("p_conv", [DEPTH, 3, 512]), 1: dout("s_conv", [DEPTH, 3, 512])}

    xres = {0: nc.dram_tensor("xres_p", [seq, D], F32, kind="Internal").ap(),
            1: nc.dram_tensor("xres_s", [DEC_SEQ, D], F32, kind="Internal").ap()}
    wb_in = nc.dram_tensor("wb_in", [DEPTH, 14, 128, 4096], BF16, kind="Internal").ap()
    wb_o = nc.dram_tensor("wb_o", [DEPTH, 3, 128, 4096], BF16, kind="Internal").ap()
    wb_out = nc.dram_tensor("wb_out", [DEPTH, 2, 128, 4096], BF16, kind="Internal").ap()

    def sb(name, shape, dt=F32):
        return nc.alloc_sbuf_tensor("sb_" + name, list(shape), dt)

    ckvT = sb("ckvT", [128, SCAP], BF16)
    kropeT = sb("kropeT", [128, SCAP], BF16)
    ckv_aug = sb("ckv_aug", [128, NBLK, 130], BF16)
    KdT = sb("KdT", [128, 4, SCAP], BF16)
    Vd = sb("Vd", [128, NBLK, 8, 66], BF16)
    Rkv = [P.res(f"kv{b}") for b in range(NBLK)]
    ident = sb("ident16", [128, 128], BF16)
    ropeb = [sb(f"rope{i}", [128, 2, 2, 16]) for i in range(2)]
    Rropeb = [P.res(f"rope{i}") for i in range(2)]
    rope_n = [0]
    btab = sb("btab", [128, 8, 33])
    masks = sb("masks16", [128, 9, 128], BF16)
    fnorm_t = sb("fnorm", [128, D])
    Rconst = P.res("const")
    vec = sb("vecs", [128, NV])
    sv = sb("svec", [128, 16])
    lrv = sb("lrv", [128, 16])
    lamv = sb("lamv", [128, 4])
    Rvec = P.res("vec")
    wuq = sb("wuq", [128, 2, 768], BF16)
    wukT = sb("wukT", [128, 8, 128], BF16)
    wuv = sb("wuv", [128, 8, 128], BF16)
    bda = sb("bda", [128, 4, 128], BF16)
    bdx = sb("bdx", [128, 4, 128], BF16)
    Rlw = P.res("layerw")

    RING = 4
    ring = [sb(f"ring{i}", [128, 4096], BF16) for i in range(RING)]
    Rring = [P.res(f"ring{i}") for i in range(RING)]
    ring_n = [0]
    xnT = sb("xnT", [128, 8, TT], BF16)
    RxnT = P.res("xnT")
    qT = sb("qT", [128, 8, TT], BF16)
    RqT = P.res("qT")
    cqnT = sb("cqnT", [128, 2, TT], BF16)
    RcqnT = P.res("cqnT")
    qlatT = [sb(f"qlatT{i}", [128, TT], BF16) for i in range(2)]
    RqlatT = [P.res(f"qlatT{i}") for i in range(2)]
    qdT = sb("qdT", [128, 4, 4 * TT], BF16)
    RqdT = P.res("qdT")
    olruT = sb("olruT", [128, 4, TT], BF16)
    RolruT = P.res("olruT")
    omlaT = sb("omlaT", [128, 4, TT], BF16)
    RomlaT = P.res("omlaT")
    odiffT = sb("odiffT", [128, 4, TT], BF16)
    RodiffT = P.res("odiffT")
    zmlaT = sb("zmlaT", [128, 4, TT], BF16)
    RzmlaT = P.res("zmlaT")
    zlruT = sb("zlruT", [128, 4, TT], BF16)
    RzlruT = P.res("zlruT")
    zdiff = sb("zdiff", [128, TT // 128, 512], BF16)
    Rzdiff = P.res("zdiff")
    NPT = 4
    PT = [sb(f"PT{i}", [128, 2 * TT], BF16) for i in range(NPT)]
    RPT = [P.res(f"PT{i}") for i in range(NPT)]
    pt_n = [0]
    xl = sb("xl", [128, 4, TT + 3])
    Rxl = P.res("xl")
    hcarry = sb("hcarry", [128, 4])
    Rhc = P.res("hcarry")
    od_raw = sb("od_raw", [128, TT // 128, 512])
    Rodraw = P.res("odraw")
    olatT = [sb(f"olatT{i}", [128, TT], BF16) for i in range(2)]
    RolatT = [P.res(f"olatT{i}") for i in range(2)]
    NSCR = 6
    scr = [sb(f"scr{i}", [128, 512]) for i in range(NSCR)]
    Rscr = [P.res(f"scr{i}") for i in range(NSCR)]
    scr_n = [0]
    g_n = [0]
    small = [sb(f"small{i}", [128, 16]) for i in range(8)]
    Rsmall = [P.res(f"small{i}") for i in range(8)]
    small_n = [0]
    xblk = [sb(f"xblk{i}", [128, D]) for i in range(2)]
    Rxblk = [P.res(f"xblk{i}") for i in range(2)]
    xblk_n = [0]
    xs16 = [sb(f"xs16_{i}", [128, D], BF16) for i in range(1)] * 2
    Rxs16 = [P.res(f"xs16_{i}") for i in range(1)] * 2
    stg = [sb(f"stg{i}", [128, 1184]) for i in range(2)]
    Rstg = [P.res(f"stg{i}") for i in range(2)]
    stg_n = [0]
    tm16 = [sb(f"tm16_{i}", [128, 768], BF16) for i in range(2)]
    Rtm16 = [P.res(f"tm16_{i}") for i in range(2)]
    tm_n = [0]

    banks = [nc.alloc_psum_tensor(f"bank{i}", [128, 512], F32) for i in range(8)]
    Rbank = [P.res(f"bank{i}", excl=True) for i in range(8)]
    bank_n = [0]

    def next_bank(lo=0, hi=8):
        i = lo + bank_n[0] % (hi - lo)
        bank_n[0] += 1
        return i

    def nxt(counter, n):
        i = counter[0] % n
        counter[0] += 1
        return i

    def T_(fn, reads, writes):
        return P.op("pe", fn, reads, writes)

    def A_(fn, reads, writes):
        return P.op("act", fn, reads, writes)

    def V_(fn, reads, writes):
        return P.op("dve", fn, reads, writes)

    def G_(fn, reads, writes):
        return P.op("pool", fn, reads, writes)

    def LD(out, in_, reads, writes):
        return P.op("sp", lambda e, o=out, i=in_: e.dma_start(out=o, in_=i), reads, writes, dma=True)

    def ST(out, in_, reads, writes, slow=False):
        return P.op("pool", lambda e, o=out, i=in_: e.dma_start(out=o, in_=i, allow_slow_non_contiguous=slow),
                    reads, writes, dma=True)

    def CAST_DMA(out, in_, reads, writes):
        return P.op("pool", lambda e, o=out, i=in_: e.dma_start(out=o, in_=i, max_dma_last_dim=4096),
                    reads, writes, dma=True)

    evac_n = [0]

    def evac(out, in_, reads, writes, scale=None, eng=None):
        if eng is None:
            eng = "act" if evac_n[0] % 2 == 0 else "dve"
            evac_n[0] += 1
        if eng == "act":
            if scale is None:
                A_(lambda e, o=out, i=in_: e.activation(out=o, in_=i, func=AF.Copy), reads, writes)
            else:
                A_(lambda e, o=out, i=in_, s=scale: e.activation(out=o, in_=i, func=AF.Copy, scale=s), reads, writes)
        else:
            if scale is None:
                V_(lambda e, o=out, i=in_: e.tensor_copy(out=o, in_=i), reads, writes)
            else:
                V_(lambda e, o=out, i=in_, s=scale: e.tensor_scalar(out=o, in0=i, scalar1=s, scalar2=None, op0=ALU.mult),
                   reads, writes)

    def mm_chain(out, pairs, reads, writes):
        n = len(pairs)
        for i, (l, r) in enumerate(pairs):
            T_(lambda e, o=out, l=l, r=r, st=(i == 0), sp=(i == n - 1): e.matmul(o, lhsT=l, rhs=r, start=st, stop=sp),
               reads, writes)

    def transpose(out, in_, rows, reads, writes):
        T_(lambda e, o=out, i=in_, r=rows: e.transpose(out=o, in_=i, identity=ident[:r, :r]), list(reads) + [Rconst], writes)

    def bf(bank_ap_tensor):
        return bank_ap_tensor[:, :].bitcast(BF16)

    def rstd_from_ssq(ssq_ap, n, rs_res):
        A_(lambda e, a=ssq_ap: e.activation(out=a, in_=a, func=AF.Ln, scale=1.0 / n, bias=EPS), [rs_res], [rs_res])
        A_(lambda e, a=ssq_ap: e.activation(out=a, in_=a, func=AF.Exp, scale=-0.5), [rs_res], [rs_res])

    CAST_DMA(ident[:, :], ident_d, [], [Rconst])
    CAST_DMA(masks[:, :, :], mask_d, [], [Rconst])
    LD(btab[:, :, :], btab_d, [], [Rconst])
    LD(fnorm_t[:, :], fnorm, [], [Rconst])
    Rwb = {}
    for l in range(depth):
        Rwb[("in", l)] = P.res(f"wbin{l}")
        Rwb[("o", l)] = P.res(f"wbo{l}")
        Rwb[("out", l)] = P.res(f"wbout{l}")
    G_(lambda e: e.memset(ckv_aug[:, :, 128:130], 1.0), [], Rkv)
    G_(lambda e: e.memset(Vd[:, :, :, 64:66], 1.0), [], Rkv)
    G_(lambda e: e.memset(qdT[:, :, :], 0.0), [], [RqdT])
    G_(lambda e: e.memset(kropeT[:, :], 0.0), [], Rkv)
    G_(lambda e: e.memset(wukT[:, :, :], 0.0), [], [Rlw])
    G_(lambda e: e.memset(qT[:, :, :], 0.0), [], [RqT])

    def cast_layer_weights(l):
        for p in range(14):
            CAST_DMA(wb_in[l, p], w_in[l, p], [], [Rwb[("in", l)]])
        for b in range(3):
            CAST_DMA(wb_o[l, b], w_o[l, b], [], [Rwb[("o", l)]])
        for eh in range(2):
            CAST_DMA(wb_out[l, eh], w_out[l, eh], [], [Rwb[("out", l)]])

    cast_layer_weights(0)

    def load_panel(kind, l, sel, slot):
        i = slot
        t = ring[i]
        if kind == "in":
            c0, nco = sel
            pidx = 0 if c0 == 0 else 1 + (c0 - 416) // 512
            LD(t[:, 0:8 * nco], wb_in[l, pidx, :, 0:8 * nco], [Rwb[("in", l)]], [Rring[i]])
            view = t[:, 0:8 * nco].rearrange("p (a n) -> p a n", a=8)
        elif kind == "o":
            LD(t[:, :], wb_o[l, sel], [Rwb[("o", l)]], [Rring[i]])
            view = t[:, :].rearrange("p (a n) -> p a n", a=4)
        else:
            LD(t[:, :], wb_out[l, sel], [Rwb[("out", l)]], [Rring[i]])
            view = t[:, :].rearrange("p (a n) -> p a n", a=8)
        return view, Rring[i]

    def layer_setup(l):
        LD(vec[:, :], vecs[l], [], [Rvec])
        LD(sv[:, :], svec[l], [], [Rvec])
        CAST_DMA(wuq[:, :, :], w_uq[l], [], [Rlw])
        CAST_DMA(wukT[0:64, :, :], w_ukT[l], [], [Rlw])
        CAST_DMA(wuv[:, :, :], w_uv[l], [], [Rlw])
        CAST_DMA(bda[:, :, :], bd_a[l], [], [Rlw])
        CAST_DMA(bdx[:, :, :], bd_x[l], [], [Rlw])
        lam_init = 0.8 - 0.6 * math.exp(-0.3 * l)
        V_(lambda e: e.tensor_tensor(out=scr[0][:, 0:32], in0=vec[:, V_LAM:V_LAM + 32],
                                     in1=vec[:, V_LAM + 32:V_LAM + 64], op=ALU.mult), [Rvec], [Rscr[0]])
        V_(lambda e: e.tensor_reduce(out=lamv[:, 1:2], in_=scr[0][:, 0:32], axis=AX.X, op=ALU.add), [Rscr[0]], [Rvec])
        V_(lambda e: e.tensor_tensor(out=scr[0][:, 32:64], in0=vec[:, V_LAM + 64:V_LAM + 96],
                                     in1=vec[:, V_LAM + 96:V_LAM + 128], op=ALU.mult), [Rvec], [Rscr[0]])
        V_(lambda e: e.tensor_reduce(out=lamv[:, 2:3], in_=scr[0][:, 32:64], axis=AX.X, op=ALU.add), [Rscr[0]], [Rvec])
        A_(lambda e: e.activation(out=lamv[:, 1:3], in_=lamv[:, 1:3], func=AF.Exp), [Rvec], [Rvec])
        V_(lambda e, li=lam_init: e.scalar_tensor_tensor(out=lamv[:, 0:1], in0=lamv[:, 2:3], scalar=-li, in1=lamv[:, 1:2],
                                                         op0=ALU.add, op1=ALU.subtract), [Rvec], [Rvec])
        mo = V_MISC
        A_(lambda e: e.activation(out=lrv[:, 8:12], in_=vec[:, mo + 12:mo + 16], func=AF.Exp, scale=-1.0), [Rvec], [Rvec])
        A_(lambda e: e.activation(out=lrv[:, 8:12], in_=lrv[:, 8:12], func=AF.Ln, bias=1.0), [Rvec], [Rvec])
        V_(lambda e: e.tensor_scalar(out=lrv[:, 0:4], in0=lrv[:, 8:12], scalar1=-8.0, scalar2=None, op0=ALU.mult), [Rvec], [Rvec])
        V_(lambda e: e.tensor_scalar(out=lrv[:, 4:8], in0=lrv[:, 8:12], scalar1=-16.0, scalar2=None, op0=ALU.mult), [Rvec], [Rvec])
        return lam_init

    def kv_build_block(kb, r, st, Rst):
        i = nxt(tm_n, 2)
        t16, Rt = tm16[i], Rtm16[i]
        evac(t16[:r, 0:672], st[:r, 0:672], [Rst], [Rt])
        G_(lambda e: e.tensor_copy(out=ckv_aug[:r, kb, 0:128], in_=st[:r, 0:128]), [Rst], [Rkv[kb]])
        G_(lambda e: e.tensor_copy(out=Vd[:r, kb, :, 0:64], in_=st[:r, 672:1184].rearrange("p (h v) -> p h v", h=8)),
           [Rst], [Rkv[kb]])
        b = next_bank()
        pb = bf(banks[b])
        transpose(pb[:, 0:r], t16[:r, 0:128], r, [Rt], [Rbank[b]])
        transpose(pb[0:32, 128:128 + r], t16[:r, 128:160], r, [Rt], [Rbank[b]])
        for g in range(4):
            transpose(pb[:, 256 + 128 * g:256 + 128 * g + r], t16[:r, 160 + 128 * g:288 + 128 * g], r, [Rt], [Rbank[b]])
        c0 = kb * 128
        evac(ckvT[:, c0:c0 + r], pb[:, 0:r], [Rbank[b]], [Rkv[kb]])
        evac(kropeT[64:96, c0:c0 + r], pb[0:32, 128:128 + r], [Rbank[b]], [Rkv[kb]])
        evac(KdT[:, :, c0:c0 + r], pb[:, 256:768].rearrange("p (g s) -> p g s", g=4)[:, :, 0:r], [Rbank[b]], [Rkv[kb]])

    def rope_tm(dst16, src, r, blk, nh, w, tq, Rsrc, Rdst):
        si = nxt(scr_n, NSCR)
        s, Rs = scr[si], Rscr[si]
        rp, Rrp = blk
        cosb = rp[:r, 0, tq, :].unsqueeze(1).to_broadcast([r, nh, 16])
        sinb = rp[:r, 1, tq, :].unsqueeze(1).to_broadcast([r, nh, 16])
        x1 = src[:, :, w - 32:w - 16]
        x2 = src[:, :, w - 16:w]
        t = [s[:r, 128 * j:128 * j + nh * 16].rearrange("p (h d) -> p h d", h=nh) for j in range(4)]
        V_(lambda e: e.tensor_tensor(out=t[0], in0=x1, in1=cosb, op=ALU.mult), [Rsrc, Rrp], [Rs])
        V_(lambda e: e.tensor_tensor(out=t[1], in0=x2, in1=sinb, op=ALU.mult), [Rsrc, Rrp], [Rs])
        V_(lambda e: e.tensor_tensor(out=t[2], in0=x2, in1=cosb, op=ALU.mult), [Rsrc, Rrp], [Rs])
        V_(lambda e: e.tensor_tensor(out=t[3], in0=x1, in1=sinb, op=ALU.mult), [Rsrc, Rrp], [Rs])
        V_(lambda e: e.tensor_tensor(out=dst16[:, :, w - 32:w - 16], in0=t[0], in1=t[1], op=ALU.subtract), [Rs], [Rdst])
        V_(lambda e: e.tensor_tensor(out=dst16[:, :, w - 16:w], in0=t[2], in1=t[3], op=ALU.add), [Rs], [Rdst])

    def attention_tile(l, b0, nt, lam_init, bgen):
        nqs = (nt + 127) // 128
        rq = [min(128, nt - 128 * j) for j in range(nqs)]
        nkb = b0 + nqs
        tot_keys = b0 * 128 + nt

        def ks_of(kb):
            return min(128, tot_keys - kb * 128)

        def col_range(kb):
            j = kb - b0
            return (128 * j if j > 0 else 0), nt

        ACC = (0, 1)

        def mk_qlat(h):
            b = 6
            mm_chain(banks[b][:, 0:nt], [(wukT[:, h, :], qT[:, h, 0:nt])], [Rlw, RqT], [Rbank[b]])
            evac(qlatT[h % 2][:, 0:nt], banks[b][:, 0:nt], [Rbank[b]], [RqlatT[h % 2]])

        mk_qlat(0)
        for h in range(8):
            qi = h % 2
            if h + 1 < 8:
                mk_qlat(h + 1)
            started = [False]
            pend = []

            def mla_score(kb):
                sbk = next_bank(2, 6)
                ks = ks_of(kb)
                c0, c1 = col_range(kb)
                mm_chain(banks[sbk][:ks, 0:c1 - c0],
                         [(ckvT[:, kb * 128:kb * 128 + ks], qlatT[qi][:, c0:c1]),
                          (kropeT[:, kb * 128:kb * 128 + ks], qT[:, h, c0:c1])],
                         [Rkv[kb], RqlatT[qi], RqT], [Rbank[sbk]])
                return sbk

            squeue = [mla_score(k) for k in range(min(3, nkb))]
            for kb in range(nkb):
                sbk = squeue.pop(0)
                if kb + 3 < nkb:
                    squeue.append(mla_score(kb + 3))
                ks = ks_of(kb)
                c0, c1 = col_range(kb)
                n = c1 - c0
                pi = nxt(pt_n, NPT)
                A_(lambda e, o=PT[pi][:ks, 0:n], i=banks[sbk][:ks, 0:n]: e.activation(out=o, in_=i, func=AF.Exp, scale=QSCALE),
                   [Rbank[sbk]], [RPT[pi]])
                if kb >= b0:
                    w = min(128, n)
                    V_(lambda e, o=PT[pi][:ks, 0:w], m=masks[:ks, 8, 0:w]: e.tensor_tensor(out=o, in0=o, in1=m, op=ALU.mult),
                       [RPT[pi], Rconst], [RPT[pi]])
                for qs in range(max(0, kb - b0), nqs):
                    ab = h % 2
                    col = qs * 129
                    st_ = not started[0]
                    started[0] = True
                    T_(lambda e, o=banks[ab][:rq[qs], col:col + 129], lt=PT[pi][:ks, qs * 128 - c0:qs * 128 - c0 + rq[qs]],
                       r_=ckv_aug[:ks, kb, 0:129], st_=st_: e.matmul(o, lhsT=lt, rhs=r_, start=st_, stop=False, skip_group_check=True),
                       [RPT[pi], Rkv[kb]], [Rbank[ab]])
            smi = nxt(small_n, 8)
            sm, Rsm = small[smi], Rsmall[smi]
            for qs in range(nqs):
                ab = h % 2
                col = qs * 129
                V_(lambda e, o=sm[:rq[qs], qs:qs + 1], i=banks[ab][:rq[qs], col + 128:col + 129]: e.reciprocal(out=o, in_=i),
                   [Rbank[ab]], [Rsm])
            ti = nxt(tm_n, 2)
            t16, Rt = tm16[ti], Rtm16[ti]
            for qs in range(nqs):
                ab = h % 2
                col = qs * 129
                evac(t16[:rq[qs], qs * 128:(qs + 1) * 128], banks[ab][:rq[qs], col:col + 128], [Rbank[ab], Rsm], [Rt],
                     scale=sm[:rq[qs], qs:qs + 1])
            tb = 6
            pb = bf(banks[tb])
            for qs in range(nqs):
                transpose(pb[:, qs * 128:qs * 128 + rq[qs]], t16[:rq[qs], qs * 128:(qs + 1) * 128], rq[qs], [Rt], [Rbank[tb]])
            oi = h % 2
            evac(olatT[oi][:, 0:nt], pb[:, 0:nt], [Rbank[tb]], [RolatT[oi]])
            ub = 7
            T_(lambda e, o=banks[ub][:, 0:nt], lt=wuv[:, h, :], r_=olatT[oi][:, 0:nt], h=h:
               e.matmul(o, lhsT=lt, rhs=r_, start=(h % 2 == 0), stop=(h % 2 == 1)), [Rlw, RolatT[oi]], [Rbank[ub]])
            if h % 2 == 1:
                V_(lambda e, o=omlaT[:, h // 2, 0:nt], i=banks[ub][:, 0:nt], z=zmlaT[:, h // 2, 0:nt]:
                   e.tensor_tensor(out=o, in0=i, in1=z, op=ALU.mult), [Rbank[ub], RzmlaT], [RomlaT])
            next(bgen, None)
            next(bgen, None)

        for _ in bgen:
            pass

        for h in range(8):
            g = h // 2
            sub = 128 if h == 0 else 256
            started = [False]

            hh = h % 2

            def d_score(kb):
                ks = ks_of(kb)
                c0, c1 = col_range(kb)
                n = c1 - c0
                sbk = next_bank(2, 6)
                T_(lambda e, o=banks[sbk][:ks, 0:2 * n].rearrange("p (m t) -> p m t", m=2),
                   lt=KdT[:, g, kb * 128:kb * 128 + ks],
                   r_=qdT[:, g, 2 * hh * TT:(2 * hh + 2) * TT].rearrange("p (m t) -> p m t", m=2)[:, :, c0:c1]:
                   e.matmul(o, lhsT=lt, rhs=r_, start=True, stop=True),
                   [Rkv[kb], RqdT], [Rbank[sbk]])
                return sbk

            slope_h = 2.0 ** (-(h + 1))
            q0 = b0 * 128
            kbl = [kb for kb in range(nkb) if slope_h * (q0 - (kb * 128 + ks_of(kb) - 1)) <= 160.0]
            squeue = [d_score(k) for k in kbl[:3]]
            for ki, kb in enumerate(kbl):
                sbk = squeue.pop(0)
                if ki + 3 < len(kbl):
                    squeue.append(d_score(kbl[ki + 3]))
                ks = ks_of(kb)
                c0, c1 = col_range(kb)
                n = c1 - c0
                pi = nxt(pt_n, NPT)
                PTv = PT[pi][:ks, 0:2 * n].rearrange("p (m t) -> p m t", m=2)
                Sv = banks[sbk][:ks, 0:2 * n].rearrange("p (m t) -> p m t", m=2)
                k0 = (c0 // sub) * sub
                while k0 < c1:
                    a0, a1 = max(k0, c0), min(k0 + sub, c1)
                    dd = b0 + (min(k0 + sub, nt) - 1) // 128 - kb
                    A_(lambda e, o=PTv[:, :, a0 - c0:a1 - c0], i=Sv[:, :, a0 - c0:a1 - c0], bb=btab[:ks, h, dd:dd + 1]:
                       e.activation(out=o, in_=i, func=AF.Exp, bias=bb, scale=DSCALE), [Rbank[sbk], Rconst], [RPT[pi]])
                    k0 += sub
                if kb >= b0:
                    w = min(128, n)
                    V_(lambda e, o=PTv[:, :, 0:w], mk=masks[:ks, h, 0:w].unsqueeze(1).to_broadcast([ks, 2, w]):
                       e.tensor_tensor(out=o, in0=o, in1=mk, op=ALU.mult), [RPT[pi], Rconst], [RPT[pi]])
                for m in range(2):
                    for qs in range(max(0, kb - b0), nqs):
                        ab = h % 2
                        col = m * 256 + qs * 65
                        st_ = not started[0]
                        started[0] = True
                        T_(lambda e, o=banks[ab][:rq[qs], col:col + 65], lt=PTv[:, m, qs * 128 - c0:qs * 128 - c0 + rq[qs]],
                           r_=Vd[:ks, kb, h, 0:65], st_=st_: e.matmul(o, lhsT=lt, rhs=r_, start=st_, stop=False, skip_group_check=True),
                           [RPT[pi], Rkv[kb]], [Rbank[ab]])
            smi = nxt(small_n, 8)
            sm, Rsm = small[smi], Rsmall[smi]
            ab = h % 2
            for m in range(2):
                V_(lambda e, o=sm[:rq[0], 4 * m:4 * m + nqs], i=banks[ab][:rq[0], m * 256:m * 256 + 65 * nqs].rearrange("p (q c) -> p q c", c=65)[:, :, 64]:
                   e.reciprocal(out=o, in_=i), [Rbank[ab]], [Rsm])
            V_(lambda e, o=sm[:rq[0], 4:4 + nqs]: e.tensor_scalar(out=o, in0=o, scalar1=lamv[:rq[0], 0:1], scalar2=None, op0=ALU.mult),
               [Rsm, Rvec], [Rsm])
            si = nxt(scr_n, NSCR)
            s, Rs = scr[si], Rscr[si]
            for qs in range(nqs):
                V_(lambda e, o=s[:rq[qs], qs * 64:(qs + 1) * 64], i=banks[ab][:rq[qs], 256 + qs * 65:256 + qs * 65 + 64], sc=sm[:rq[qs], 4 + qs:5 + qs]:
                   e.tensor_scalar(out=o, in0=i, scalar1=sc, scalar2=None, op0=ALU.mult), [Rbank[ab], Rsm], [Rs])
                V_(lambda e, o=od_raw[:rq[qs], qs, h * 64:(h + 1) * 64], i=banks[ab][:rq[qs], qs * 65:qs * 65 + 64],
                   sc=sm[:rq[qs], qs:qs + 1], t=s[:rq[qs], qs * 64:(qs + 1) * 64]:
                   e.scalar_tensor_tensor(out=o, in0=i, scalar=sc, in1=t, op0=ALU.mult, op1=ALU.add), [Rbank[ab], Rsm, Rs], [Rodraw])
            next(bgen, None)
        def post():
            for qs in range(nqs):
                diff_post_qs(qs)

        def diff_post_qs(qs):
            r = rq[qs]
            s, Rs = scr[5], Rscr[5]
            smi = nxt(small_n, 8)
            sm, Rsm = small[smi], Rsmall[smi]
            V_(lambda e, o=s[:r, :], i=od_raw[:r, qs, :]: e.tensor_tensor(out=o, in0=i, in1=i, op=ALU.mult), [Rodraw], [Rs])
            V_(lambda e, o=sm[:r, 0:8], i=s[:r, :].rearrange("p (h v) -> p h v", h=8): e.tensor_reduce(out=o, in_=i, axis=AX.X, op=ALU.add),
               [Rs], [Rsm])
            rstd_from_ssq(sm[:r, 0:8], 64, Rsm)
            V_(lambda e, o=s[:r, :].rearrange("p (h v) -> p h v", h=8), i=od_raw[:r, qs, :].rearrange("p (h v) -> p h v", h=8),
               rb=sm[:r, 0:8].unsqueeze(2).to_broadcast([r, 8, 64]): e.tensor_tensor(out=o, in0=i, in1=rb, op=ALU.mult),
               [Rodraw, Rsm], [Rs])
            ti = nxt(tm_n, 2)
            t16, Rt = tm16[ti], Rtm16[ti]
            V_(lambda e, o=t16[:r, 0:512], i=s[:r, :], z=zdiff[:r, qs, :]: e.tensor_tensor(out=o, in0=i, in1=z, op=ALU.mult),
               [Rs, Rzdiff], [Rt])
            tb = next_bank(6, 8)
            pb = bf(banks[tb])
            for c in range(4):
                transpose(pb[:, c * 128:c * 128 + r], t16[:r, c * 128:(c + 1) * 128], r, [Rt], [Rbank[tb]])
            evac(odiffT[:, :, qs * 128:qs * 128 + r], pb[:, 0:512].rearrange("p (c t) -> p c t", c=4)[:, :, 0:r], [Rbank[tb]], [RodiffT])

        return post

    def tile_compute(l, sq, b0, nt, first_tile, last_tile, lam_init, xsrc):
        nb = (nt + 127) // 128
        rows = [min(128, nt - 128 * j) for j in range(nb)]
        tok0 = 0 if sq == 1 else b0 * 128

        ri = nxt(rope_n, 2)
        ropet, Rropet = ropeb[ri], Rropeb[ri]
        LD(ropet[:, :, 0:nb, :], rope_d[:, :, b0:b0 + nb, :], [], [Rropet])
        pA, RpA = load_panel("in", l, (0, 416), 0)
        pz, Rpz = load_panel("in", l, (C_ZM, 512), 2)

        for j in range(nb):
            r = rows[j]
            xi = nxt(xblk_n, 2)
            xb, Rxb = xblk[xi], Rxblk[xi]
            LD(xb[:r, :], xsrc[tok0 + 128 * j:tok0 + 128 * j + r, :], [Rxres[sq][b0 - (16 if sq else 0) + j]] if l > 0 else [], [Rxb])
            smi = nxt(small_n, 8)
            sm, Rsm = small[smi], Rsmall[smi]
            x16, Rx16 = xs16[xi], Rxs16[xi]
            A_(lambda e, i=xb[:r, :], a=sm[:r, 0:1], jk=x16[:r, :]: e.activation(out=jk, in_=i, func=AF.Square, accum_out=a),
               [Rxb], [Rsm, Rx16])
            rstd_from_ssq(sm[:r, 0:1], D, Rsm)
            V_(lambda e, o=x16[:r, :], i=xb[:r, :], s=sm[:r, 0:1]: e.tensor_scalar(out=o, in0=i, scalar1=s, scalar2=None, op0=ALU.mult),
               [Rxb, Rsm], [Rx16])
            b = next_bank()
            pb = bf(banks[b])
            for kc in range(8):
                transpose(pb[:, kc * 128:kc * 128 + r], x16[:r, kc * 128:(kc + 1) * 128], r, [Rx16], [Rbank[b]])
            for kc in range(8):
                evac(xnT[:, kc, 128 * j:128 * j + r], pb[:, kc * 128:kc * 128 + r], [Rbank[b], Rvec], [RxnT],
                     scale=vec[:, V_G + kc:V_G + kc + 1])

        def tm_proj(panel, pres, j, ncols, cofs=0):
            r = rows[j]
            b = next_bank()
            mm_chain(banks[b][:r, 0:ncols], [(xnT[:, kc, 128 * j:128 * j + r], panel[:, kc, cofs:cofs + ncols]) for kc in range(8)],
                     [RxnT, pres], [Rbank[b]])
            return b

        stgs = []
        bA_ = [tm_proj(pA, RpA, j, 416) for j in range(nb)]
        for j in range(nb):
            r = rows[j]
            b = bA_[j]
            si = nxt(stg_n, 2)
            st, Rst = stg[si], Rstg[si]
            stgs.append((st, Rst))
            smi = nxt(small_n, 8)
            sm, Rsm = small[smi], Rsmall[smi]
            ti = nxt(tm_n, 2)
            t16, Rt = tm16[ti], Rtm16[ti]
            A_(lambda e, i=banks[b][:r, 0:256], a=sm[:r, 0:1], jk=t16[:r, 0:256]: e.activation(out=jk, in_=i, func=AF.Square, accum_out=a),
               [Rbank[b]], [Rsm, Rt])
            A_(lambda e, i=banks[b][:r, 256:384], a=sm[:r, 1:2], jk=t16[:r, 256:384]: e.activation(out=jk, in_=i, func=AF.Square, accum_out=a),
               [Rbank[b]], [Rsm, Rt])
            rstd_from_ssq(sm[:r, 0:1], 256, Rsm)
            rstd_from_ssq(sm[:r, 1:2], 128, Rsm)
            V_(lambda e, o=t16[:r, 0:256], i=banks[b][:r, 0:256], s=sm[:r, 0:1]: e.tensor_scalar(out=o, in0=i, scalar1=s, scalar2=None, op0=ALU.mult),
               [Rbank[b], Rsm], [Rt])
            V_(lambda e, o=st[:r, 0:128], i=banks[b][:r, 256:384], s=sm[:r, 1:2], gg=vec[:r, V_KVG:V_KVG + 128]:
               e.scalar_tensor_tensor(out=o, in0=i, scalar=s, in1=gg, op0=ALU.mult, op1=ALU.mult), [Rbank[b], Rsm, Rvec], [Rst])
            rope_tm(st[:r, 128:160].rearrange("p (h w) -> p h w", h=1), banks[b][:r, 384:416].rearrange("p (h w) -> p h w", h=1),
                    r, (ropet, Rropet), 1, 32, j, Rbank[b], Rst)
            tb = next_bank()
            pb = bf(banks[tb])
            for kc in range(2):
                transpose(pb[:, kc * 128:kc * 128 + r], t16[:r, kc * 128:(kc + 1) * 128], r, [Rt], [Rbank[tb]])
            for kc in range(2):
                evac(cqnT[:, kc, 128 * j:128 * j + r], pb[:, kc * 128:kc * 128 + r], [Rbank[tb], Rvec], [RcqnT],
                     scale=vec[:, V_QG + kc:V_QG + kc + 1])
        for c in range(4):
            b = next_bank()
            mm_chain(banks[b][:, 0:nt], [(pz[:, kc, c * 128:(c + 1) * 128], xnT[:, kc, 0:nt]) for kc in range(8)], [RxnT, Rpz], [Rbank[b]])
            A_(lambda e, o=zmlaT[:, c, 0:nt], i=banks[b][:, 0:nt]: e.activation(out=o, in_=i, func=AF.Silu), [Rbank[b]], [RzmlaT])
        pq, Rpq = load_panel("in", l, (C_QD, 512), 3)
        bq_ = [tm_proj(pq, Rpq, j, 512) for j in range(nb)]
        for j in range(nb):
            r = rows[j]
            b = bq_[j]
            ti = nxt(tm_n, 2)
            t16, Rt = tm16[ti], Rtm16[ti]
            evac(t16[:r, 0:512], banks[b][:r, 0:512], [Rbank[b]], [Rt])
            tb = next_bank()
            pb = bf(banks[tb])
            for g in range(4):
                transpose(pb[:, g * 128:g * 128 + r], t16[:r, g * 128:(g + 1) * 128], r, [Rt], [Rbank[tb]])
            for pgp in range(4):
                evac(qdT[32 * pgp:32 * pgp + 32, :, pgp * TT + 128 * j:pgp * TT + 128 * j + r],
                     pb[32 * pgp:32 * pgp + 32, 0:512].rearrange("p (g t) -> p g t", g=4)[:, :, 0:r], [Rbank[tb]], [RqdT])
        pk, Rpk = load_panel("in", l, (C_KD, 512), 1)
        for j in range(nb):
            r = rows[j]
            b = tm_proj(pk, Rpk, j, 512)
            evac(stgs[j][0][:r, 160:672], banks[b][:r, 0:512], [Rbank[b]], [stgs[j][1]])
        pv, Rpv = load_panel("in", l, (C_VD, 512), 0)
        for j in range(nb):
            r = rows[j]
            b = tm_proj(pv, Rpv, j, 512)
            st, Rst = stgs[j]
            evac(st[:r, 672:1184], banks[b][:r, 0:512], [Rbank[b]], [Rst])
            t0 = tok0 + 128 * j
            ST(o_ckv[sq][l, t0:t0 + r, :], st[:r, 0:128], [Rst], [])
            ST(o_kr[sq][l, t0:t0 + r, :], st[:r, 128:160], [Rst], [])
            ST(o_dk[sq][l, t0:t0 + r, :], st[:r, 160:672], [Rst], [])
            ST(o_dv[sq][l, t0:t0 + r, :], st[:r, 672:1184], [Rst], [])
            kv_build_block(b0 + j, r, st, Rst)
        pzd, Rpzd = load_panel("in", l, (C_ZD, 512), 2)
        for j in range(nb):
            r = rows[j]
            b = tm_proj(pzd, Rpzd, j, 512)
            si = nxt(scr_n, NSCR)
            s, Rs = scr[si], Rscr[si]
            A_(lambda e, o=s[:r, :], i=banks[b][:r, 0:512]: e.activation(out=o, in_=i, func=AF.Silu), [Rbank[b]], [Rs])
            V_(lambda e, o=zdiff[:r, j, :], i=s[:r, :], gg=vec[:r, V_SUB:V_SUB + 512], li=lam_init:
               e.scalar_tensor_tensor(out=o, in0=i, scalar=1.0 - li, in1=gg, op0=ALU.mult, op1=ALU.mult), [Rs, Rvec], [Rzdiff])
        bqm = []
        for j in range(nb):
            r = rows[j]
            bA = next_bank()
            bB = next_bank()
            mm_chain(banks[bA][:r, 0:480], [(cqnT[:, kc, 128 * j:128 * j + r], wuq[:, kc, 0:480]) for kc in range(2)], [RcqnT, Rlw], [Rbank[bA]])
            mm_chain(banks[bB][:r, 0:288], [(cqnT[:, kc, 128 * j:128 * j + r], wuq[:, kc, 480:768]) for kc in range(2)], [RcqnT, Rlw], [Rbank[bB]])
            bqm.append((bA, bB))
        for j in range(nb):
            r = rows[j]
            bA, bB = bqm[j]
            ti = nxt(tm_n, 2)
            t16, Rt = tm16[ti], Rtm16[ti]
            for (bb, h0, nh) in ((bA, 0, 5), (bB, 5, 3)):
                src = banks[bb][:r, 0:nh * 96].rearrange("p (h w) -> p h w", h=nh)
                dst = t16[:r, h0 * 96:(h0 + nh) * 96].rearrange("p (h w) -> p h w", h=nh)
                evac(dst[:, :, 0:64], src[:, :, 0:64], [Rbank[bb]], [Rt])
                rope_tm(dst, src, r, (ropet, Rropet), nh, 96, j, Rbank[bb], Rt)
            tb = next_bank()
            pb = bf(banks[tb])
            for h in range(8):
                transpose(pb[0:96, h * 128:h * 128 + r], t16[:r, h * 96:(h + 1) * 96], r, [Rt], [Rbank[tb]])
            evac(qT[0:96, :, 128 * j:128 * j + r], pb[0:96, :].rearrange("p (h t) -> p h t", h=8)[:, :, 0:r], [Rbank[tb]], [RqT])
        if first_tile:
            if sq == 0:
                V_(lambda e: e.memset(xl[:, :, 0:3], 0.0), [], [Rxl])
                V_(lambda e: e.memset(hcarry[:, :], 0.0), [], [Rhc])
            else:
                V_(lambda e: e.tensor_copy(out=xl[:, :, 0:3], in_=sv[:, 4:16].rearrange("p (c k) -> p c k", c=4)), [Rvec], [Rxl])
                V_(lambda e: e.tensor_copy(out=hcarry[:, :], in_=sv[:, 0:4]), [Rvec], [Rhc])
        pxl, Rpxl = load_panel("in", l, (C_XL, 512), 3)
        for c in range(4):
            b = next_bank()
            mm_chain(banks[b][:, 0:nt], [(pxl[:, kc, c * 128:(c + 1) * 128], xnT[:, kc, 0:nt]) for kc in range(8)], [RxnT, Rpxl], [Rbank[b]])
            evac(xl[:, c, 3:3 + nt], banks[b][:, 0:nt], [Rbank[b]], [Rxl])
        pzl, Rpzl = load_panel("in", l, (C_ZL, 512), 1)
        for c in range(4):
            b = next_bank()
            mm_chain(banks[b][:, 0:nt], [(pzl[:, kc, c * 128:(c + 1) * 128], xnT[:, kc, 0:nt]) for kc in range(8)], [RxnT, Rpzl], [Rbank[b]])
            A_(lambda e, o=zlruT[:, c, 0:nt], i=banks[b][:, 0:nt]: e.activation(out=o, in_=i, func=AF.Silu), [Rbank[b]], [RzlruT])

        def lru_gen():
            mo = V_MISC

            def hb(i):
                if i < 12:
                    return scr[i // 2][:, (i % 2) * 256:(i % 2) * 256 + nt], Rscr[i // 2]
                j = i - 12
                return od_raw[:, j // 2, (j % 2) * 256:(j % 2) * 256 + nt], Rodraw

            XC = [hb(4 * c) for c in range(4)]
            AA = [hb(4 * c + 1) for c in range(4)]
            BB = [hb(4 * c + 2) for c in range(4)]
            TTb = [hb(4 * c + 3) for c in range(4)]
            for c in range(4):
                xc, Rxc = XC[c]
                V_(lambda e, o=xc, i=xl[:, c, 0:nt], w=vec[:, V_CW + 4 * c:V_CW + 4 * c + 1], bb=vec[:, mo + c:mo + c + 1]:
                   e.tensor_scalar(out=o, in0=i, scalar1=w, scalar2=bb, op0=ALU.mult, op1=ALU.add), [Rxl, Rvec], [Rxc])
                for k in range(1, 4):
                    V_(lambda e, o=xc, i=xl[:, c, k:k + nt], w=vec[:, V_CW + 4 * c + k:V_CW + 4 * c + k + 1]:
                       e.scalar_tensor_tensor(out=o, in0=i, scalar=w, in1=o, op0=ALU.mult, op1=ALU.add), [Rxl, Rvec, Rxc], [Rxc])
                xc16, Rxc16 = PT[c], RPT[c]
                V_(lambda e, o=xc16[:, 0:nt], i=xc: e.tensor_copy(out=o, in_=i), [Rxc], [Rxc16])
                bk = 6
                T_(lambda e, o=banks[bk][:, 0:nt], lt=bda[:, c, :], r_=xc16[:, 0:nt]: e.matmul(o, lhsT=lt, rhs=r_, start=True, stop=True),
                   [Rlw, Rxc16], [Rbank[bk]])
                T_(lambda e, o=banks[bk][:, 256:256 + nt], lt=bdx[:, c, :], r_=xc16[:, 0:nt]: e.matmul(o, lhsT=lt, rhs=r_, start=True, stop=True),
                   [Rlw, Rxc16], [Rbank[bk]])
                A_(lambda e, o=AA[c][0], i=banks[bk][:, 0:nt], bb=vec[:, mo + 4 + c:mo + 5 + c]: e.activation(out=o, in_=i, func=AF.Sigmoid, bias=bb),
                   [Rbank[bk], Rvec], [AA[c][1]])
                A_(lambda e, o=BB[c][0], i=banks[bk][:, 256:256 + nt], bb=vec[:, mo + 8 + c:mo + 9 + c]: e.activation(out=o, in_=i, func=AF.Sigmoid, bias=bb),
                   [Rbank[bk], Rvec], [BB[c][1]])
                yield
            for c in range(4):
                A_(lambda e, o=TTb[c][0], i=AA[c][0], s_=lrv[:, 4 + c:5 + c]: e.activation(out=o, in_=i, func=AF.Exp, scale=s_), [AA[c][1], Rvec], [TTb[c][1]])
                A_(lambda e, o=AA[c][0], s_=lrv[:, c:c + 1]: e.activation(out=o, in_=o, func=AF.Exp, scale=s_), [AA[c][1], Rvec], [AA[c][1]])
                if c % 2 == 1:
                    yield
            for c in range(4):
                A_(lambda e, o=TTb[c][0]: e.activation(out=o, in_=o, func=AF.Ln, scale=-1.0, bias=1.0), [TTb[c][1]], [TTb[c][1]])
            for c in range(4):
                A_(lambda e, o=TTb[c][0]: e.activation(out=o, in_=o, func=AF.Exp, scale=0.5), [TTb[c][1]], [TTb[c][1]])
            yield
            for c in range(4):
                t_, Rt_ = TTb[c]
                b_, Rb = BB[c]
                a_, Ra = AA[c]
                xc, Rxc = XC[c]
                if first_tile and sq == 0:
                    V_(lambda e, o=t_[:, 0:1]: e.memset(o, 1.0), [Rt_], [Rt_])
                V_(lambda e, o=b_, i=t_: e.tensor_tensor(out=o, in0=o, in1=i, op=ALU.mult), [Rb, Rt_], [Rb])
                V_(lambda e, o=b_, i=xc: e.tensor_tensor(out=o, in0=o, in1=i, op=ALU.mult), [Rb, Rxc], [Rb])
                V_(lambda e, o=t_, d0=a_, d1=b_, ini=hcarry[:, c:c + 1]:
                   e.tensor_tensor_scan(out=o, data0=d0, data1=d1, initial=ini, op0=ALU.mult, op1=ALU.add), [Ra, Rb, Rhc], [Rt_])
                V_(lambda e, o=hcarry[:, c:c + 1], i=t_[:, nt - 1:nt]: e.tensor_copy(out=o, in_=i), [Rt_], [Rhc])
                V_(lambda e, o=olruT[:, c, 0:nt], i=t_, z=zlruT[:, c, 0:nt]: e.tensor_tensor(out=o, in0=i, in1=z, op=ALU.mult),
                   [Rt_, RzlruT], [RolruT])
                yield
            if last_tile:
                for c in range(4):
                    ST(o_conv[sq][l][:, c * 128:(c + 1) * 128].rearrange("k p -> p k"), xl[:, c, nt:nt + 3], [Rxl], [], slow=True)
                ST(o_h[sq][l].rearrange("(c p) -> p c", p=128), hcarry[:, :], [Rhc], [], slow=True)
            else:
                V_(lambda e: e.tensor_copy(out=xl[:, :, 0:3], in_=xl[:, :, nt:nt + 3]), [Rxl], [Rxl])

            yield

        XS = (0, 3, 0)
        pX = {0: load_panel("o", l, 0, XS[0])}
        pY = {(0, 0): load_panel("in", l, (C_G, 512), 1), (0, 1): load_panel("in", l, (C_G + 512, 512), 2)}
        pX[1] = load_panel("o", l, 1, XS[1])

        bgen = lru_gen()
        diff_post = attention_tile(l, b0, nt, lam_init, bgen)
        for _ in bgen:
            pass

        xres_blk = []
        for j in range(nb):
            r = rows[j]
            xi = nxt(xblk_n, 2)
            xb, Rxb = xblk[xi], Rxblk[xi]
            LD(xb[:r, :], xsrc[tok0 + 128 * j:tok0 + 128 * j + r, :], [Rxres[sq][b0 - (16 if sq else 0) + j]] if l > 0 else [], [Rxb])
            xres_blk.append((xb, Rxb))

        mergedT, RmT = qT, RqT
        obT = ((omlaT, RomlaT), (odiffT, RodiffT), (olruT, RolruT))

        def hbw(i):
            return scr[i // 2][:, (i % 2) * 256:(i % 2) * 256 + nt], Rscr[i // 2]

        pw = {}
        for br_ in range(3):
            if br_ == 1:
                diff_post()
            po, Rpo = pX[br_]
            oT, RoT = obT[br_]
            for half in range(2):
                pg, Rpg = pY[(br_, half)]
                for e4 in range(4):
                    ec = half * 4 + e4
                    bg = next_bank()
                    bo = next_bank()
                    mm_chain(banks[bg][:, 0:nt], [(pg[:, kc, e4 * 128:(e4 + 1) * 128], xnT[:, kc, 0:nt]) for kc in range(8)],
                             [RxnT, Rpg], [Rbank[bg]])
                    mm_chain(banks[bo][:, 0:nt], [(po[:, wc, ec * 128:(ec + 1) * 128], oT[:, wc, 0:nt]) for wc in range(4)],
                             [RoT, Rpo], [Rbank[bo]])
                    gs, Rg = hbw(8 + nxt(g_n, 2))
                    A_(lambda e, o=gs, i=banks[bg][:, 0:nt]: e.activation(out=o, in_=i, func=AF.Sigmoid), [Rbank[bg]], [Rg])
                    ma, Rma = hbw(ec)
                    if br_ == 0:
                        V_(lambda e, o=ma, g_=gs, i=banks[bo][:, 0:nt]: e.tensor_tensor(out=o, in0=g_, in1=i, op=ALU.mult),
                           [Rg, Rbank[bo]], [Rma])
                    else:
                        V_(lambda e, g_=gs, i=banks[bo][:, 0:nt]: e.tensor_tensor(out=g_, in0=g_, in1=i, op=ALU.mult),
                           [Rg, Rbank[bo]], [Rg])
                        if br_ == 1:
                            V_(lambda e, o=ma, g_=gs: e.tensor_tensor(out=o, in0=o, in1=g_, op=ALU.add), [Rg, Rma], [Rma])
                        else:
                            V_(lambda e, o=mergedT[:, ec, 0:nt], m_=ma, g_=gs: e.tensor_tensor(out=o, in0=m_, in1=g_, op=ALU.add),
                               [Rg, Rma], [RmT])
                if half == 0:
                    if br_ + 1 <= 2:
                        pY[(br_ + 1, 0)] = load_panel("in", l, (C_G + (br_ + 1) * 1024, 512), 1)
                    else:
                        pw[1] = load_panel("out", l, 1, 1)
                else:
                    if br_ + 1 <= 2:
                        pY[(br_ + 1, 1)] = load_panel("in", l, (C_G + (br_ + 1) * 1024 + 512, 512), 2)
                    if br_ + 2 <= 2:
                        pX[br_ + 2] = load_panel("o", l, br_ + 2, XS[br_ + 2])
                    elif br_ == 1:
                        pw[0] = load_panel("out", l, 0, 3)
        for j in range(nb):
            r = rows[j]
            xb, Rxb = xres_blk[j]
            for eh in range(2):
                b = next_bank()
                mm_chain(banks[b][:r, :], [(mergedT[:, dc, 128 * j:128 * j + r], pw[eh][0][:, dc, :]) for dc in range(8)],
                         [RmT, pw[eh][1]], [Rbank[b]])
                V_(lambda e, o=xb[:r, eh * 512:(eh + 1) * 512], i=banks[b][:r, :]: e.tensor_tensor(out=o, in0=o, in1=i, op=ALU.add),
                   [Rbank[b], Rxb], [Rxb])
            t0 = tok0 + 128 * j
            rx = Rxres[sq][b0 - (16 if sq else 0) + j]
            if l < depth - 1:
                ST(xres[sq][t0:t0 + r, :], xb[:r, :], [Rxb], [rx])
            else:
                smi = nxt(small_n, 8)
                sm, Rsm = small[smi], Rsmall[smi]
                A_(lambda e, i=xb[:r, :], a=sm[:r, 0:1], jk=xs16[0][:r, :]: e.activation(out=jk, in_=i, func=AF.Square, accum_out=a),
                   [Rxb], [Rsm, Rxs16[0]])
                rstd_from_ssq(sm[:r, 0:1], D, Rsm)
                V_(lambda e, o=xb[:r, :], s=sm[:r, 0:1], gg=fnorm_t[:r, :]: e.scalar_tensor_tensor(out=o, in0=o, scalar=s, in1=gg, op0=ALU.mult, op1=ALU.mult),
                   [Rxb, Rsm, Rconst], [Rxb])
                ST((y_s if sq else y_p)[t0:t0 + r, :], xb[:r, :], [Rxb], [])

    Rxres = {0: [P.res(f"xrp{b}") for b in range(seq // 128)], 1: [P.res("xrs")]}

    for l in range(depth):
        if l + 1 < depth:
            cast_layer_weights(l + 1)
        lam_init = layer_setup(l)
        ntiles = seq // TT
        for i in range(ntiles):
            tile_compute(l, 0, i * (TT // 128), TT, i == 0, i == ntiles - 1, lam_init, x_p if l == 0 else xres[0])
        if do_sample:
            for kb in range(PAST // 128):
                si = nxt(stg_n, 2)
                st, Rst = stg[si], Rstg[si]
                LD(st[:, 0:128], c_ckv[l, kb * 128:(kb + 1) * 128, :], [], [Rst])
                LD(st[:, 128:160], c_kr[l, kb * 128:(kb + 1) * 128, :], [], [Rst])
                LD(st[:, 160:672], c_dk[l, kb * 128:(kb + 1) * 128, :], [], [Rst])
                LD(st[:, 672:1184], c_dv[l, kb * 128:(kb + 1) * 128, :], [], [Rst])
                kv_build_block(kb, 128, st, Rst)
            tile_compute(l, 1, PAST // 128, DEC_SEQ, True, True, lam_init, x_s if l == 0 else xres[1])
    stats = P.emit()
    return nc, stats


def _consts():
    half = 16
    inv = (10000.0 ** (-np.arange(half, dtype=np.float32) / half)).astype(np.float32)
    pos = (np.arange(32)[None, :] * 128 + np.arange(128)[:, None]).astype(np.float32)
    ang = pos[:, :, None] * inv[None, None, :]
    rope = np.zeros((128, 2, 32, 16), np.float32)
    rope[:, 0] = np.cos(ang)
    rope[:, 1] = np.sin(ang)
    slopes = 2.0 ** (-8.0 * np.arange(1, 9, dtype=np.float64) / 8)
    p = np.arange(128, dtype=np.float64)
    d = np.arange(33, dtype=np.float64)
    btab = -(slopes[None, :, None] * (127.0 - p[:, None, None] + 128.0 * d[None, None, :]))
    s = np.arange(128)[:, None]
    t = np.arange(128)[None, :]
    vis = (s // 64) <= (t // 64)
    masks = np.zeros((128, 9, 128), np.float64)
    for h in range(8):
        m = np.where(s > t, np.exp(-2.0 * slopes[h] * (s - t)), 1.0)
        masks[:, h, :] = np.where(vis, m, 0.0)
    masks[:, 8, :] = vis.astype(np.float64)
    return rope, btab.astype(np.float32), masks.astype(np.float32), np.eye(128, dtype=np.float32)


def _col(v, n):
    return np.ascontiguousarray(v.reshape(n, 128).T)


def _panels_in(w):
    L = w.shape[0]
    wk = w.reshape(L, 8, 128, NIN).transpose(0, 2, 1, 3)
    out = np.zeros((L, 14, 128, 4096), np.float32)
    out[:, 0, :, 0:8 * 416] = wk[:, :, :, 0:416].reshape(L, 128, 8 * 416)
    for p in range(13):
        out[:, 1 + p] = wk[:, :, :, 416 + 512 * p:416 + 512 * (p + 1)].reshape(L, 128, 4096)
    return out


def _pad_uv(w):
    L = w.shape[0]
    out = np.zeros((L, 128, 8, 128), np.float32)
    for h in range(8):
        out[:, :, h, 64 * (h % 2):64 * (h % 2) + 64] = w[:, :, h, :]
    return out


def _prep_shared(inp):
    L = DEPTH
    rope, btab, masks, ident = _consts()
    vecs = np.zeros((L, 128, NV), np.float32)
    for l in range(L):
        vecs[l, :, V_G:V_G + 8] = _col(inp["norm_g"][l], 8)
        vecs[l, :, V_QG:V_QG + 2] = _col(inp["mla_q_norm"][l], 2)
        vecs[l, :, V_KVG:V_KVG + 128] = inp["mla_kv_norm"][l][None, :]
        vecs[l, :, V_SUB:V_SUB + 512] = np.tile(inp["diff_subln"][l], 8)[None, :]
        for i, k in enumerate(("diff_lq1", "diff_lk1", "diff_lq2", "diff_lk2")):
            vecs[l, :, V_LAM + 32 * i:V_LAM + 32 * (i + 1)] = inp[k][l][None, :]
        cw = inp["lru_conv_w"][l]
        for c in range(4):
            for k in range(4):
                vecs[l, :, V_CW + 4 * c + k] = cw[k, c * 128:(c + 1) * 128]
        vecs[l, :, V_MISC + 0:V_MISC + 4] = _col(inp["lru_conv_b"][l], 4)
        vecs[l, :, V_MISC + 4:V_MISC + 8] = _col(inp["lru_b_a"][l], 4)
        vecs[l, :, V_MISC + 8:V_MISC + 12] = _col(inp["lru_b_x"][l], 4)
        vecs[l, :, V_MISC + 12:V_MISC + 16] = _col(inp["lru_lambda"][l], 4)

    def bd(w):
        out = np.zeros((L, 128, 4, 128), np.float32)
        for c in range(4):
            for nl in range(2):
                out[:, nl * 64:(nl + 1) * 64, c, nl * 64:(nl + 1) * 64] = w[:, 2 * c + nl]
        return out

    sh = {
        "vecs": vecs,
        "fnorm": np.ascontiguousarray(np.broadcast_to(inp["final_norm"][None, :], (128, D))).astype(np.float32),
        "w_in": _panels_in(inp["w_in"]),
        "w_o": np.ascontiguousarray(np.stack([inp["w_o_mla"], inp["w_o_diff"], inp["w_o_lru"]], 1)
                                    .reshape(L, 3, 4, 128, D).transpose(0, 1, 3, 2, 4)).reshape(L, 3, 128, 4096),
        "w_out": np.ascontiguousarray(inp["w_out"].reshape(L, 8, 128, 2, 512).transpose(0, 3, 2, 1, 4)).reshape(L, 2, 128, 4096),
        "w_uq": np.ascontiguousarray(inp["mla_w_uq"].reshape(L, 2, 128, 768).transpose(0, 2, 1, 3)),
        "w_ukT": np.ascontiguousarray(inp["mla_w_uk"].transpose(0, 3, 2, 1)),
        "w_uv": _pad_uv(inp["mla_w_uv"]),
        "bd_a": bd(inp["lru_w_a"]),
        "bd_x": bd(inp["lru_w_x"]),
        "ident": ident, "rope": rope, "btab": btab, "masks": masks,
    }
    return sh


_CACHE = {}


def kernel(**inputs):
    inp = {k: np.asarray(v) for k, v in inputs.items()}
    if "nc" not in _CACHE:
        _CACHE["nc"] = build_program()[0]
    nc = _CACHE["nc"]
    sh = _prep_shared(inp)
    in_maps = []
    for c in range(8):
        m = dict(sh)
        m["x_p"] = np.ascontiguousarray(inp["x_prompt"][c % 4])
        m["x_s"] = np.ascontiguousarray(inp["x_sample"][c])
        m["c_ckv"] = np.ascontiguousarray(inp["cache_mla_ckv"][:, c])
        m["c_kr"] = np.ascontiguousarray(inp["cache_mla_krope"][:, c])
        m["c_dk"] = np.ascontiguousarray(inp["cache_diff_k"][:, c].reshape(DEPTH, PAST, 512))
        m["c_dv"] = np.ascontiguousarray(inp["cache_diff_v"][:, c].reshape(DEPTH, PAST, 512))
        sv = np.zeros((DEPTH, 128, 16), np.float32)
        for l in range(DEPTH):
            sv[l, :, 0:4] = _col(inp["state_lru_h"][l, c], 4)
            cv = inp["state_lru_conv"][l, c]
            for ch in range(4):
                for k in range(3):
                    sv[l, :, 4 + 3 * ch + k] = cv[k, ch * 128:(ch + 1) * 128]
        m["svec"] = sv
        in_maps.append(m)
    res = run_bass_kernel_spmd(nc, in_maps, core_ids=list(range(8)))
    R = res.results
    B = 4
    y_prompt = np.stack([R[b]["y_p"] for b in range(B)], 0)
    y_sample = np.stack([R[c]["y_s"] for c in range(8)], 0)

    def pst(key, tail):
        return np.stack([R[b][key] for b in range(B)], 1).reshape((DEPTH, B) + tail)

    def sst(key, tail):
        return np.stack([R[c][key] for c in range(8)], 1).reshape((DEPTH, 8) + tail)

    outs = (y_prompt, y_sample,
            pst("p_ckv", (SEQ, 128)), pst("p_kr", (SEQ, 32)), pst("p_dk", (SEQ, 8, 64)), pst("p_dv", (SEQ, 8, 64)),
            pst("p_h", (512,)), pst("p_conv", (3, 512)),
            sst("s_ckv", (DEC_SEQ, 128)), sst("s_kr", (DEC_SEQ, 32)), sst("s_dk", (DEC_SEQ, 8, 64)), sst("s_dv", (DEC_SEQ, 8, 64)),
            sst("s_h", (512,)), sst("s_conv", (3, 512)))
    return tuple(np.ascontiguousarray(o, dtype=np.float32) for o in outs)
```
